# Optimizing a Trainium2 kernel written in Bass

```python
import jax, jax.numpy as jnp
from jax import lax
import numpy as np

D_MODEL = 2048
BATCH = 4
SEQ = 4096
DEPTH = 4

CHUNK = 64

FOX_WIDTH = D_MODEL // 2
FOX_HEAD_DIM = 128
FOX_HEADS = FOX_WIDTH // FOX_HEAD_DIM
Q_BLOCK = 128
FORGET_BIAS_MEAN = 2.0

SGU_WIDTH = D_MODEL // 2
SGU_GROUP_DIM = 128
SGU_GROUPS = SGU_WIDTH // SGU_GROUP_DIM
SGU_CHUNK = 128

D_FF = 5632
CONV_WIDTH = 3

RMS_EPS = 1e-6

IN_WIDTH = 3 * FOX_WIDTH + FOX_HEADS + 2 * SGU_WIDTH + 2 * D_MODEL

kernel_name = "hybrid_fox_sgu_convffn_trunk"


def rmsnorm(x, gain):
    xf = x.astype(jnp.float32)
    inv = lax.rsqrt(jnp.mean(xf * xf, axis=-1, keepdims=True) + RMS_EPS)
    return (xf * inv).astype(x.dtype) * gain


def split_in_proj(proj):
    sizes = (FOX_WIDTH, FOX_WIDTH, FOX_WIDTH, FOX_HEADS,
             SGU_WIDTH, SGU_WIDTH, D_MODEL, D_MODEL)
    points = tuple(int(p) for p in np.cumsum(sizes)[:-1])
    return jnp.split(proj, points, axis=-1)


def forgetting_attention(q, k, v, log_f):
    B, S, H, Dh = q.shape
    n_blk = S // Q_BLOCK
    scale = Dh ** -0.5
    c = jnp.cumsum(log_f, axis=1).transpose(0, 2, 1)
    qb = q.reshape(B, n_blk, Q_BLOCK, H, Dh).transpose(1, 0, 2, 3, 4)
    cb = c.reshape(B, H, n_blk, Q_BLOCK).transpose(2, 0, 1, 3)
    kpos = jnp.arange(S)

    def block(args):
        i, qi, ci = args
        s = jnp.einsum('bqhd,bkhd->bhqk', qi, k,
                       preferred_element_type=jnp.float32) * scale
        s = s + ci[..., :, None] - c[:, :, None, :]
        qpos = i * Q_BLOCK + jnp.arange(Q_BLOCK)
        mask = kpos[None, :] <= qpos[:, None]
        s = jnp.where(mask, s, -1e30)
        p = jax.nn.softmax(s, axis=-1).astype(v.dtype)
        return jnp.einsum('bhqk,bkhd->bqhd', p, v)

    o = lax.map(block, (jnp.arange(n_blk), qb, cb))
    return o.transpose(1, 0, 2, 3, 4).reshape(B, S, H * Dh)


def spatial_gating(u, v, g_norm, w_s, b_s):
    B, S, _ = v.shape
    n = S // SGU_CHUNK
    vc = rmsnorm(v, g_norm).reshape(B, n, SGU_CHUNK, SGU_GROUPS, SGU_GROUP_DIM)
    w = jnp.tril(w_s)
    mixed = jnp.einsum('gts,bnsgc->bntgc', w, vc) + b_s.T[None, None, :, :, None]
    return u * mixed.reshape(B, S, SGU_WIDTH)


def conv_ffn(h, w_up, conv_w, conv_b, w_down):
    S = h.shape[1]
    a, b = jnp.split(h @ w_up, 2, axis=-1)
    a_pad = jnp.pad(a, ((0, 0), (CONV_WIDTH - 1, 0), (0, 0)))
    acc = conv_b + conv_w[0] * a_pad[:, 0:S]
    for tap in range(1, CONV_WIDTH):
        acc = acc + conv_w[tap] * a_pad[:, tap:tap + S]
    return (jax.nn.gelu(acc) * b) @ w_down


def setup_inputs(seed: int = 0) -> dict:
    key = jax.random.key(seed)
    ks = jax.random.split(key, 16)
    f32 = jnp.float32

    def nrm(k, shape, fan_in):
        return jax.random.normal(k, shape, f32) * (fan_in ** -0.5)

    def gain(k, shape):
        return 1.0 + 0.01 * jax.random.normal(k, shape, f32)

    return {
        "x": jax.random.normal(ks[0], (BATCH, SEQ, D_MODEL), f32),
        "g_mix": gain(ks[1], (DEPTH, D_MODEL)),
        "w_in": nrm(ks[2], (DEPTH, D_MODEL, IN_WIDTH), D_MODEL),
        "b_forget": FORGET_BIAS_MEAN + 0.5 * jax.random.normal(ks[3], (DEPTH, FOX_HEADS), f32),
        "g_sgu": gain(ks[4], (DEPTH, SGU_WIDTH)),
        "w_spatial": nrm(ks[5], (DEPTH, SGU_GROUPS, SGU_CHUNK, SGU_CHUNK), SGU_CHUNK),
        "b_spatial": 1.0 + 0.1 * jax.random.normal(ks[6], (DEPTH, SGU_GROUPS, SGU_CHUNK), f32),
        "w_branch_a": nrm(ks[7], (DEPTH, FOX_WIDTH, D_MODEL), FOX_WIDTH),
        "w_branch_b": nrm(ks[8], (DEPTH, SGU_WIDTH, D_MODEL), SGU_WIDTH),
        "w_out": nrm(ks[9], (DEPTH, D_MODEL, D_MODEL), D_MODEL),
        "g_ffn": gain(ks[10], (DEPTH, D_MODEL)),
        "w_up": nrm(ks[11], (DEPTH, D_MODEL, 2 * D_FF), D_MODEL),
        "conv_w": nrm(ks[12], (DEPTH, CONV_WIDTH, D_FF), CONV_WIDTH),
        "conv_b": 0.01 * jax.random.normal(ks[13], (DEPTH, D_FF), f32),
        "w_down": nrm(ks[14], (DEPTH, D_FF, D_MODEL), D_FF),
        "g_final": gain(ks[15], (D_MODEL,)),
    }


def reference(x, g_mix, w_in, b_forget, g_sgu, w_spatial, b_spatial,
              w_branch_a, w_branch_b, w_out, g_ffn, w_up, conv_w, conv_b,
              w_down, g_final):
    B, S, _ = x.shape
    for l in range(DEPTH):
        h = rmsnorm(x, g_mix[l])
        q, k, v, f_logit, u, vg, gate_a, gate_b = split_in_proj(h @ w_in[l])
        log_f = jax.nn.log_sigmoid((f_logit + b_forget[l]).astype(jnp.float32))
        y_a = forgetting_attention(
            q.reshape(B, S, FOX_HEADS, FOX_HEAD_DIM),
            k.reshape(B, S, FOX_HEADS, FOX_HEAD_DIM),
            v.reshape(B, S, FOX_HEADS, FOX_HEAD_DIM),
            log_f)
        y_b = spatial_gating(jax.nn.gelu(u), jax.nn.gelu(vg),
                             g_sgu[l], w_spatial[l], b_spatial[l])
        merged = (jax.nn.sigmoid(gate_a) * (y_a @ w_branch_a[l])
                  + jax.nn.sigmoid(gate_b) * (y_b @ w_branch_b[l]))
        x = x + merged @ w_out[l]
        x = x + conv_ffn(rmsnorm(x, g_ffn[l]), w_up[l], conv_w[l], conv_b[l], w_down[l])
    return rmsnorm(x, g_final)
```

```python
import contextlib
import numpy as np
import concourse.bass as bass
import concourse.mybir as mybir
from concourse.bass_utils import run_bass_kernel_spmd

F32 = mybir.dt.float32
BF16 = mybir.dt.bfloat16
AF = mybir.ActivationFunctionType
ALU = mybir.AluOpType

D = 2048
SEQ = 4096
DEPTH = 4
H = 8
DH = 128
FW = 1024
DFF = 5632
IN_W = 3 * FW + H + 2 * FW + 2 * D
KC = D // 128
FC = DFF // 128
TS = 2048
EPS = 1e-6
OQ, OK_, OV, OF, OU, OVG, OGA, OGB = 0, 1024, 2048, 3072, 3080, 4104, 5128, 7176

SAME_ENG_SYNC = True
SEM_ROT = 30000


class Tk:
    __slots__ = ("name", "w", "rs", "dsem", "dcnt", "dkey", "depoch")

    def __init__(self, name=""):
        self.name = name
        self.w = None
        self.rs = {}
        self.dsem = None
        self.dcnt = 0
        self.dkey = None
        self.depoch = 0


class Eng:
    def __init__(self, K, name, handle, kind):
        self.K = K
        self.name = name
        self.h = handle
        self.kind = kind
        self.epoch = 0
        self.sem = K.new_sem("e_%s_0" % name)
        self.key = "%s#0" % name
        self.n = 0
        self.seen = {}
        self.prog = []
        self.pending = False

    def _rot(self):
        if self.n >= SEM_ROT and not self.pending:
            self.epoch += 1
            self.sem = self.K.new_sem("e_%s_%d" % (self.name, self.epoch))
            self.key = "%s#%d" % (self.name, self.epoch)
            self.n = 0

    def _wait(self, ev, war=False):
        if ev is None:
            return
        sem, val, key, owner = ev
        if owner is self:
            if war or self.kind in ("pe", "sp") or not SAME_ENG_SYNC:
                return
        if self.seen.get(key, 0) >= val:
            return
        self.seen[key] = val
        self.prog.append(lambda h, s=sem, v=val: h.wait_ge(s, v))

    def deps(self, reads, writes):
        for t in reads:
            self._wait(t.w)
        for t in writes:
            self._wait(t.w, war=True)
            for r in t.rs.values():
                self._wait(r, war=True)

    def op(self, fn, reads=(), writes=(), inc=True):
        self.deps(reads, writes)
        self._rot()
        ev = (self.sem, self.n + 1, self.key, self)
        self.pending = not inc
        if inc:
            self.n += 1
            sem = self.sem
            self.prog.append(lambda h, f=fn, s=sem: f(h).then_inc(s, 1))
        else:
            self.prog.append(lambda h, f=fn: f(h))
        for t in reads:
            t.rs[self.key] = ev
        for t in writes:
            t.w = ev
            t.rs = {}
        return ev

    def dma(self, pairs, slot, reads=(), writes=()):
        self.deps(reads, writes)
        K = self.K
        if slot.dsem is None or slot.dcnt + 16 * len(pairs) > SEM_ROT:
            slot.depoch += 1
            slot.dsem = K.new_sem("d_%s_%d" % (slot.name, slot.depoch))
            slot.dkey = "d_%s#%d#%d" % (slot.name, id(slot), slot.depoch)
            slot.dcnt = 0
        sem = slot.dsem
        for (o, i) in pairs:
            slot.dcnt += 16
            self.prog.append(lambda h, o=o, i=i, s=sem: h.dma_start(out=o, in_=i).then_inc(s, 16))
        ev = (sem, slot.dcnt, slot.dkey, None)
        K.dirty[slot.dkey] = ev
        for t in reads:
            t.rs[slot.dkey] = ev
        for t in writes:
            t.w = ev
            t.rs = {}
        return ev


def _eng_cc(self, groups, in_ap, out_ap, slot, reads=(), writes=()):
    self.deps(reads, writes)
    K = self.K
    if slot.dsem is None:
        slot.depoch += 1
        slot.dsem = K.new_sem("c_%s_%d" % (slot.name, slot.depoch))
        slot.dkey = "c_%s#%d#%d" % (slot.name, id(slot), slot.depoch)
        slot.dcnt = 0
    sem = slot.dsem
    slot.dcnt += 1
    self.prog.append(lambda h, s=sem: h.collective_compute(
        "AllGather", ALU.bypass, replica_groups=groups, ins=[in_ap.opt()], outs=[out_ap.opt()]).then_inc(s, 1))
    ev = (sem, slot.dcnt, slot.dkey, None)
    K.dirty[slot.dkey] = ev
    for t in reads:
        t.rs[slot.dkey] = ev
    for t in writes:
        t.w = ev
        t.rs = {}
    return ev


Eng.cc = _eng_cc
PAIRS = [[0, 1], [2, 3], [4, 5], [6, 7]]


class Kern:
    def __init__(self):
        self.nc = bass.Bass("TRN2", target_bir_lowering=False)
        self.stack = contextlib.ExitStack()
        self.nsem = 0
        self.dirty = {}
        self.pe = Eng(self, "pe", self.nc.tensor, "pe")
        self.act = Eng(self, "act", self.nc.scalar, "act")
        self.dve = Eng(self, "dve", self.nc.vector, "dve")
        self.pool = Eng(self, "pool", self.nc.gpsimd, "pool")
        self.sp = Eng(self, "sp", self.nc.sync, "sp")
        self.engs = [self.pe, self.act, self.dve, self.pool, self.sp]
        self.banks = []
        self.bank_i = 0
        self.abank_i = 0

    def new_sem(self, name):
        self.nsem += 1
        return self.stack.enter_context(self.nc.semaphore(name))

    def dram(self, name, shape, dtype, kind="Internal"):
        return self.nc.dram_tensor(name, list(shape), dtype, kind=kind).ap()

    def sbuf(self, name, shape, dtype):
        return self.stack.enter_context(self.nc.sbuf_tensor(name, list(shape), dtype))

    def make_banks(self):
        for i in range(8):
            t = self.stack.enter_context(self.nc.psum_tensor("bank%d" % i, [128, 512], F32))
            self.banks.append((t, Tk("bank%d" % i)))

    def bank(self):
        b = self.banks[self.bank_i % 4]
        self.bank_i += 1
        return b

    def abank(self):
        b = self.banks[4 + self.abank_i % 4]
        self.abank_i += 1
        return b

    def barrier(self):
        evs = [(e.sem, e.n, e.key, None) for e in self.engs if e.n > 0]
        evs += list(self.dirty.values())
        self.dirty = {}
        for e in self.engs:
            for ev in evs:
                if ev[2] == e.key:
                    continue
                e._wait(ev)

    def finish(self):
        self.barrier()
        nc = self.nc
        with nc.Block() as block:
            @block.tensor
            def _(e):
                for f in self.pe.prog:
                    f(e)

            @block.scalar
            def _(e):
                for f in self.act.prog:
                    f(e)

            @block.vector
            def _(e):
                for f in self.dve.prog:
                    f(e)

            @block.gpsimd
            def _(e):
                for f in self.pool.prog:
                    f(e)

            @block.sync
            def _(e):
                for f in self.sp.prog:
                    f(e)
        self.stack.close()
        return nc


class Rot:
    def __init__(self, K, name, n, shape, dtype):
        self.items = []
        for i in range(n):
            self.items.append((K.sbuf("%s%d" % (name, i), shape, dtype), Tk("%s%d" % (name, i))))
        self.i = 0

    def next(self):
        it = self.items[self.i % len(self.items)]
        self.i += 1
        return it


def build(L, S, final_norm=True, debug=False):
    K = Kern()
    nc = K.nc
    pe, act, dve, pool, sp = K.pe, K.act, K.dve, K.pool, K.sp
    NSEG = S // TS
    NBLK = S // 128
    NT = TS // 512
    NB = TS // 128

    EI = "ExternalInput"
    xT_in = K.dram("xT", [D, S], F32, EI)
    w_in = K.dram("w_in", [L, D, IN_W], F32, EI)
    w_ba = K.dram("w_branch_a", [L, FW, D], F32, EI)
    w_bb = K.dram("w_branch_b", [L, FW, D], F32, EI)
    w_out = K.dram("w_out", [L, D, D], F32, EI)
    w_up = K.dram("w_up", [L, D, 2 * DFF], F32, EI)
    w_down = K.dram("w_down", [L, DFF, D], F32, EI)
    gmix_h = K.dram("gmix_h", [128, L * KC], F32, EI)
    gffn_h = K.dram("gffn_h", [128, L * KC], F32, EI)
    gfin_h = K.dram("gfin_h", [128, KC], F32, EI)
    convw_h = K.dram("convw_h", [128, L * 3 * FC], F32, EI)
    convb_h = K.dram("convb_h", [128, L * FC], F32, EI)
    bfor_h = K.dram("bfor_h", [128, L * H], F32, EI)
    gsgu_h = K.dram("gsgu_h", [L, 128, FW], F32, EI)
    bsp_h = K.dram("bsp_h", [L, 128, H * 128], F32, EI)
    wsT_h = K.dram("wsT_h", [L, 128, H * 128], F32, EI)
    outT = K.dram("outT", [D, S], F32, "ExternalOutput")
    dbg = K.dram("dbg", [L * 4 * 128, 128], F32, "ExternalOutput") if debug else None
    tkDBG = Tk("dbg")
    dbgx = K.dram("dbgx", [L * 2 * D, 128], F32, "ExternalOutput") if debug else None

    X = K.dram("X", [D, S], F32)
    QT = K.dram("QT", [FW, TS], BF16)
    pm_h = K.dram("pm", [128, 16], F32, EI)
    NBUF = 1
    KTl = [[K.dram("KTl%d_%d" % (l, c), [256, TS], BF16) for c in range(4)] for l in range(NBUF)] * L
    KTg = [[K.dram("KTg%d_%d" % (l, c), [512, TS], BF16) for c in range(4)] for l in range(NBUF)] * L
    Vl = [[K.dram("Vl%d_%d" % (l, c), [512, FW], BF16) for c in range(4)] for l in range(NBUF)] * L
    Vg = [[K.dram("Vg%d_%d" % (l, c), [1024, FW], BF16) for c in range(4)] for l in range(NBUF)] * L
    CRl = [K.dram("CRl%d" % l, [128, 128], F32) for l in range(NBUF)] * L
    CRg = [K.dram("CRg%d" % l, [256, 128], F32) for l in range(NBUF)] * L
    XTl = [K.dram("XTl%d" % l, [128, 32], F32) for l in range(NBUF)] * L
    XTg = [K.dram("XTg%d" % l, [256, 32], F32) for l in range(NBUF)] * L
    GA = K.dram("GA", [D, TS], BF16)
    GB = K.dram("GB", [D, TS], BF16)
    GT = K.dram("GT", [DFF, TS], BF16)
    tkX = [[Tk("X%d_%d" % (n, t)) for t in range(S // 512)] for n in range(KC)]
    tkQT = [Tk("QT%d" % n) for n in range(H)]
    tkKT = [Tk("KT%d" % n) for n in range(H)]
    tkV = Tk("V")
    tkKTg = [[Tk("KTg%d_%d" % (l, c)) for c in range(4)] for l in range(NBUF)] * L
    tkVg = [[Tk("Vg%d_%d" % (l, c)) for c in range(4)] for l in range(NBUF)] * L
    tkVc = [Tk("Vc%d" % c) for c in range(4)]
    tkCRl = [Tk("CRl%d" % l) for l in range(NBUF)] * L
    tkCRg = [Tk("CRg%d" % l) for l in range(NBUF)] * L
    tkXTl = [Tk("XTl%d" % l) for l in range(NBUF)] * L
    tkXTg = [Tk("XTg%d" % l) for l in range(NBUF)] * L
    tkGA = [Tk("GA%d" % n) for n in range(KC)]
    tkGB = [Tk("GB%d" % n) for n in range(KC)]
    tkGT = [Tk("GT%d" % n) for n in range(FC)]

    K.make_banks()
    ARENA = K.sbuf("arena", [128, 65536], BF16)
    tkA = Tk("arena")

    def a_bf(off, n):
        return ARENA[:, off // 2: off // 2 + n]

    def a_f32(off, n):
        return ARENA[:, off // 2: off // 2 + 2 * n].bitcast(F32)

    R2 = 65536
    HT = a_bf(0, KC * TS).rearrange("p (k t) -> p k t", k=KC)
    UT = a_bf(R2, H * TS).rearrange("p (k t) -> p k t", k=H)
    VCN = a_bf(R2 + 32768, NB * FW).rearrange("p (b f) -> p b f", b=NB)
    YA = a_bf(R2 + 32768, H * TS).rearrange("p (k t) -> p k t", k=H)
    YB = UT
    MERGED = HT
    XS = [a_f32(R2 + i * 16384, KC * 256).rearrange("p (k t) -> p k t", k=KC) for i in range(2)]
    SQ = a_bf(R2 + 32768, KC * 256).rearrange("p (k t) -> p k t", k=KC)
    tkXS = [Tk("XS0"), Tk("XS1")]
    tkSQ = Tk("SQ")
    tkHT = Tk("HT")
    tkUT = Tk("UT")
    tkVCN = Tk("VCN")
    tkYA = tkVCN
    ATQ = [a_bf(i * 4096, TS) for i in range(2)]
    ATK = [a_bf(8192 + i * 8192, 2 * TS) for i in range(2)]
    ATV = [a_bf(24576 + i * 8192, 32 * 128).rearrange("p (b d) -> p b d", b=32) for i in range(2)]
    tkATQ = [Tk("ATQ0"), Tk("ATQ1")]
    tkATK = [Tk("ATK0"), Tk("ATK1")]
    tkATV = [Tk("ATV0"), Tk("ATV1")]
    tkATKp = [Tk("ATKp0"), Tk("ATKp1")]
    tkATVp = [Tk("ATVp0"), Tk("ATVp1")]
    FU = 90112
    ASB = [a_f32(FU + i * 8208, 2 + TS) for i in range(2)]
    tkASB = [Tk("ASB0"), Tk("ASB1")]
    ACC = [a_f32(FU + 16416 + i * 2048, 512) for i in range(2)]
    tkACC = [Tk("ACC0"), Tk("ACC1")]
    GG = [a_f32(FU + 20512 + i * 2048, 512) for i in range(2)]
    tkGG = [Tk("GG0"), Tk("GG1")]
    GTH = a_bf(0, 22 * TS).rearrange("p (k t) -> p k t", k=22)
    tkGTH = Tk("GTH")

    WS = Rot(K, "wslot", 2, [128, 8192], BF16)
    OT = Rot(K, "ot", 6, [128, 512], BF16)
    GTL = OT
    PT = Rot(K, "pt", 4, [128, 512], BF16)
    T32 = Rot(K, "t32", 6, [128, 512], F32)
    XT = T32
    VGF = Rot(K, "vgf", 2, [128, FW], F32)
    SM = Rot(K, "sm", 8, [128, 16], F32)

    ones_bf = K.sbuf("ones_bf", [128, 128], BF16)
    mask_bf = K.sbuf("mask_bf", [128, 128], BF16)
    uneg = K.sbuf("uneg", [128, 128], F32)
    onesneg = K.sbuf("onesneg", [128, 128], F32)
    halfneg = K.sbuf("halfneg", [128, 128], F32)
    eps_t = K.sbuf("eps_t", [128, 1], F32)
    one_t = K.sbuf("one_t", [128, 1], F32)
    gmix = K.sbuf("gmix", [128, KC], F32)
    gffn = K.sbuf("gffn", [128, KC], F32)
    gfin = K.sbuf("gfin", [128, KC], F32)
    convw = K.sbuf("convw", [128, 3 * FC], F32)
    convb = K.sbuf("convb", [128, FC], F32)
    bfor = K.sbuf("bfor", [128, H], F32)
    gsgu = K.sbuf("gsgu", [128, FW], F32)
    bsp = K.sbuf("bsp", [128, H * 128], F32)
    wst_f = VGF.items[0][0]
    wst = K.sbuf("wst", [128, H * 128], BF16)
    wf = K.sbuf("wf", [128, KC * H], BF16)
    call = K.sbuf("call", [128, NBLK * H], F32)
    cref = K.sbuf("cref", [128, NBLK * H], F32)
    lsum = [K.sbuf("lsum%d" % i, [128, H], F32) for i in range(2)]
    crel = K.sbuf("crel", [128, NBLK * H], F32)
    crelg = K.sbuf("crelg", [128, NBLK * H], F32)
    pm_sb = K.sbuf("pm_sb", [128, 16], F32)
    pmask = pm_sb[:, 0:1]
    pflag = pm_sb[:, 1:2]
    xtl_sb = K.sbuf("xtl_sb", [128, 32], F32)
    xtg_sb = K.sbuf("xtg_sb", [128, 32], F32)
    hx = K.sbuf("hx", [128, 32], BF16)
    tkCREL = Tk("crel")
    tkCRELG = Tk("crelg")
    tkXTL = Tk("xtl")
    tkXTGs = Tk("xtg")
    tkHX = Tk("hx")
    biasb = Rot(K, "biasb", 4, [128, 16], F32)
    tkC = Tk("consts")
    tkL = Tk("layerc")
    tkWF = Tk("wf")
    tkCALL = Tk("call")
    tkLS = [Tk("ls0"), Tk("ls1")]

    CALL3 = call[:].rearrange("p (b h) -> p b h", h=H)
    CREF3 = cref[:].rearrange("p (b h) -> p b h", h=H)
    CREL3 = crel[:].rearrange("p (b h) -> p b h", h=H)
    CRELG3 = crelg[:].rearrange("p (b h) -> p b h", h=H)
    HX3 = hx[:].rearrange("p (k t) -> p k t", k=KC)

    pool.op(lambda h: h.memset(ones_bf[:], 1.0), writes=[tkC])
    pool.op(lambda h: h.memset(mask_bf[:], 1.0), writes=[tkC])
    pool.op(lambda h: h.memset(uneg[:], -1.0), writes=[tkC])
    pool.op(lambda h: h.memset(onesneg[:], -1.0), writes=[tkC])
    pool.op(lambda h: h.memset(halfneg[:], -1.0), writes=[tkC])
    pool.op(lambda h: h.memset(eps_t[:], EPS), writes=[tkC])
    pool.op(lambda h: h.memset(one_t[:], 1.0), writes=[tkC])
    pool.op(lambda h: h.affine_select(out=mask_bf[:], in_=mask_bf[:], pattern=[[1, 128]], compare_op=ALU.is_ge,
                                      fill=0.0, base=0, channel_multiplier=-1), writes=[tkC])
    pool.op(lambda h: h.affine_select(out=uneg[:], in_=uneg[:], pattern=[[1, 128]], compare_op=ALU.is_ge,
                                      fill=0.0, base=0, channel_multiplier=-1), writes=[tkC])
    pool.op(lambda h: h.affine_select(out=halfneg[:], in_=halfneg[:], pattern=[[0, 128]], compare_op=ALU.is_ge,
                                      fill=0.0, base=63, channel_multiplier=-1), writes=[tkC])
    sp.dma([(pm_sb[:], pm_h)], tkC, writes=[tkC])
    sp.dma([(gfin[:], gfin_h)], tkC, writes=[tkC])
    tkX0 = Tk("xcopy")
    allX = [t for row in tkX for t in row]
    sp.dma([(X[i * 512:(i + 1) * 512, :], xT_in[i * 512:(i + 1) * 512, :]) for i in range(4)], tkX0, writes=allX)
    K.barrier()

    def wload(pairs):
        slot, tk = WS.next()
        pool.dma([(f(slot), src) for (f, src) in pairs], tk, writes=[tk])
        return slot, tk

    def wview(slot, kc, ncols):
        return slot[:, 0:kc * ncols].rearrange("p (k n) -> p k n", k=kc)

    def wsrc(w2d, r0, nk, c0, ncols):
        return w2d[r0:r0 + nk * 128, c0:c0 + ncols].rearrange("(k p) n -> p k n", p=128)

    def mm_group(ps, ps_tk, lhs_list, rhs_list, reads, cols=None):
        n = len(lhs_list)
        out = ps[:] if cols is None else ps[:, cols]
        for i in range(n):
            pe.op(lambda h, o=out, a=lhs_list[i], b=rhs_list[i], st=(i == 0), sp_=(i == n - 1):
                  h.matmul(o, lhsT=a, rhs=b, start=st, stop=sp_),
                  reads=reads, writes=[ps_tk], inc=(i == n - 1))

    def resid_epilogue(n, gt, ps, ps_tk):
        xt, xtk = XT.next()
        rows = slice(n * 128, (n + 1) * 128)
        cols = slice(gt * 512, (gt + 1) * 512)
        sp.dma([(xt[:], X[rows, cols])], xtk, reads=[tkX[n][gt]], writes=[xtk])
        dve.op(lambda h: h.tensor_tensor(out=xt[:], in0=ps[:], in1=xt[:], op=ALU.add),
               reads=[ps_tk, xtk], writes=[xtk])
        sp.dma([(X[rows, cols], xt[:])], xtk, reads=[xtk], writes=[tkX[n][gt]])

    def norm_stage(seg, gvec, goff, to_out=False):
        for st in range(TS // 256):
            t0 = seg * TS + st * 256
            gt = t0 // 512
            xs, xtk = XS[st % 2], tkXS[st % 2]
            sp.dma([(xs, X[:, t0:t0 + 256].rearrange("(k p) t -> p k t", p=128))], xtk,
                   reads=[tkX[n][gt] for n in range(KC)], writes=[xtk])
            act.op(lambda h, xs=xs: h.activation(out=SQ, in_=xs, func=AF.Square), reads=[xtk], writes=[tkSQ])
            ps, ps_tk = K.bank()
            mm_group(ps, ps_tk, [ones_bf[:]] * KC, [SQ[:, k, :] for k in range(KC)], [tkSQ, tkC], cols=slice(0, 256))
            rs, rtk = T32.next()
            act.op(lambda h, ps=ps, rs=rs: h.activation(out=rs[:, 0:256], in_=ps[:, 0:256], func=AF.Sqrt,
                                                        bias=eps_t[:], scale=1.0 / D), reads=[ps_tk, tkC], writes=[rtk])
            dve.op(lambda h, rs=rs: h.reciprocal(out=rs[:, 256:512], in_=rs[:, 0:256]), reads=[rtk], writes=[rtk])
            for k in range(KC):
                if to_out:
                    dve.op(lambda h, k=k, xs=xs, rs=rs: h.scalar_tensor_tensor(
                        out=xs[:, k, :], in0=xs[:, k, :], scalar=gvec[:, goff + k:goff + k + 1], in1=rs[:, 256:512],
                        op0=ALU.mult, op1=ALU.mult), reads=[xtk, rtk, tkC, tkL], writes=[xtk])
                else:
                    dve.op(lambda h, k=k, xs=xs, rs=rs, st=st: h.scalar_tensor_tensor(
                        out=HT[:, k, st * 256:(st + 1) * 256], in0=xs[:, k, :], scalar=gvec[:, goff + k:goff + k + 1],
                        in1=rs[:, 256:512], op0=ALU.mult, op1=ALU.mult), reads=[xtk, rtk, tkC, tkL], writes=[tkHT])
            if to_out:
                sp.dma([(outT[:, t0:t0 + 256].rearrange("(k p) t -> p k t", p=128), xs)], xtk, reads=[xtk])

    def gemm_fm(w2d, c0, nchunks, kc, in_view, in_tks, epilogue, seg, group=4, r0=0):
        for g0 in range(0, nchunks, group):
            ng = min(group, nchunks - g0)
            ncols = ng * 128
            slot, stk = wload([(lambda s, ncols=ncols: wview(s, kc, ncols), wsrc(w2d, r0, kc, c0 + g0 * 128, ncols))])
            wv = wview(slot, kc, ncols)
            for j in range(ng):
                n = g0 + j
                for tt in range(NT):
                    ps, ps_tk = K.bank()
                    mm_group(ps, ps_tk, [wv[:, k, j * 128:(j + 1) * 128] for k in range(kc)],
                             [in_view(k, tt) for k in range(kc)], [stk] + in_tks)
                    epilogue(n, tt, ps, ps_tk)

    for l in range(L):
        Wi = w_in[l]
        for pair in [(gsgu[:], gsgu_h[l]), (bsp[:], bsp_h[l]), (wst_f[:], wsT_h[l]),
                     (gmix[:], gmix_h[:, l * KC:(l + 1) * KC]), (gffn[:], gffn_h[:, l * KC:(l + 1) * KC]),
                     (convw[:], convw_h[:, l * 3 * FC:(l + 1) * 3 * FC]), (convb[:], convb_h[:, l * FC:(l + 1) * FC]),
                     (bfor[:], bfor_h[:, l * H:(l + 1) * H])]:
            sp.dma([pair], tkL, writes=[tkL])
        pool.op(lambda h: h.affine_select(out=wst_f[:].rearrange("p (g t) -> p g t", g=H),
                                          in_=wst_f[:].rearrange("p (g t) -> p g t", g=H),
                                          pattern=[[0, H], [1, 128]], compare_op=ALU.is_ge, fill=0.0, base=0,
                                          channel_multiplier=-1), reads=[tkL], writes=[tkL])
        dve.op(lambda h: h.tensor_copy(out=wst[:], in_=wst_f[:]), reads=[tkL], writes=[tkL])
        pool.dma([(wf[:].rearrange("p (k n) -> p k n", k=KC), wsrc(Wi, 0, KC, OF, H))], tkWF, writes=[tkWF])
        dve.op(lambda h: h.memset(lsum[0][:], 0.0), writes=[tkLS[0]])
        ls_i = 0
        WST3 = wst[:].rearrange("p (g t) -> p g t", g=H)
        BSP3 = bsp[:].rearrange("p (g t) -> p g t", g=H)

        for seg in range(NSEG):
            T0 = seg * TS
            norm_stage(seg, gmix, 0)
            K.barrier()

            hin = lambda k, tt: HT[:, k, tt * 512:(tt + 1) * 512]

            def ep_q(n, tt, ps, ps_tk):
                ot, otk = OT.next()
                act.op(lambda h: h.activation(out=ot[:], in_=ps[:], func=AF.Copy, scale=DH ** -0.5),
                       reads=[ps_tk], writes=[otk])
                sp.dma([(QT[n * 128:(n + 1) * 128, tt * 512:(tt + 1) * 512], ot[:])], otk, reads=[otk], writes=[tkQT[n]])

            def ep_k(n, tt, ps, ps_tk):
                ot, otk = OT.next()
                dve.op(lambda h: h.tensor_copy(out=ot[:], in_=ps[:]), reads=[ps_tk], writes=[otk])
                sp.dma([(KTl[l][n // 2][(n % 2) * 128:(n % 2 + 1) * 128, tt * 512:(tt + 1) * 512], ot[:])], otk,
                       reads=[otk], writes=[tkKT[n]])

            def ep_u(n, tt, ps, ps_tk):
                act.op(lambda h: h.activation(out=UT[:, n, tt * 512:(tt + 1) * 512], in_=ps[:], func=AF.Gelu_apprx_tanh),
                       reads=[ps_tk], writes=[tkUT])

            def mk_ep_gate(G, tks):
                def ep(n, tt, ps, ps_tk):
                    ot, otk = OT.next()
                    act.op(lambda h: h.activation(out=ot[:], in_=ps[:], func=AF.Sigmoid), reads=[ps_tk], writes=[otk])
                    sp.dma([(G[n * 128:(n + 1) * 128, tt * 512:(tt + 1) * 512], ot[:])], otk, reads=[otk], writes=[tks[n]])
                return ep

            gemm_fm(Wi, OK_, H, KC, hin, [tkHT], ep_k, seg)
            for c in range(4):
                pool.cc(PAIRS, KTl[l][c], KTg[l][c], tkKTg[l][c], reads=[tkKT[2 * c], tkKT[2 * c + 1]], writes=[tkKTg[l][c]])
            for cg in range(2):
                slot, stk = wload([(lambda s: wview(s, KC, 512), wsrc(Wi, 0, KC, OV + cg * 512, 512))])
                wv = wview(slot, KC, 512)
                for b in range(NB):
                    ps, ps_tk = K.bank()
                    mm_group(ps, ps_tk, [HT[:, k, b * 128:(b + 1) * 128] for k in range(KC)],
                             [wv[:, k, :] for k in range(KC)], [stk, tkHT])
                    ot, otk = OT.next()
                    dve.op(lambda h, ot=ot, ps=ps: h.tensor_copy(out=ot[:], in_=ps[:]), reads=[ps_tk], writes=[otk])
                    sp.dma([(Vl[l][b // 4][(b % 4) * 128:(b % 4 + 1) * 128, cg * 512:(cg + 1) * 512], ot[:])], otk,
                           reads=[otk], writes=[tkVc[b // 4]])
            for c in range(4):
                pool.cc(PAIRS, Vl[l][c], Vg[l][c], tkVg[l][c], reads=[tkVc[c]], writes=[tkVg[l][c]])
            gemm_fm(Wi, OQ, H, KC, hin, [tkHT], ep_q, seg)
            gemm_fm(Wi, OU, H, KC, hin, [tkHT], ep_u, seg)
            gemm_fm(Wi, OGA, KC, KC, hin, [tkHT], mk_ep_gate(GA, tkGA), seg)
            gemm_fm(Wi, OGB, KC, KC, hin, [tkHT], mk_ep_gate(GB, tkGB), seg)
            slotA, stkA = wload([(lambda s: wview(s, KC, 512), wsrc(Wi, 0, KC, OVG, 512))])
            slotB, stkB = wload([(lambda s: wview(s, KC, 512), wsrc(Wi, 0, KC, OVG + 512, 512))])
            wvA, wvB = wview(slotA, KC, 512), wview(slotB, KC, 512)
            for b in range(NB):
                vg, vtk = VGF.next()
                for (wv, stk, c0) in ((wvA, stkA, 0), (wvB, stkB, 512)):
                    ps, ps_tk = K.bank()
                    mm_group(ps, ps_tk, [HT[:, k, b * 128:(b + 1) * 128] for k in range(KC)],
                             [wv[:, k, :] for k in range(KC)], [stk, tkHT])
                    act.op(lambda h, vg=vg, ps=ps, c0=c0: h.activation(out=vg[:, c0:c0 + 512], in_=ps[:], func=AF.Gelu_apprx_tanh),
                           reads=[ps_tk], writes=[vtk])
                sm, smk = SM.next()
                junk, jtk = T32.next()
                act.op(lambda h, vg=vg, sm=sm, junk=junk: h.activation(out=junk[:], in_=vg[:, 0:512], func=AF.Square,
                                                                       accum_out=sm[:, 0:1]), reads=[vtk], writes=[jtk, smk])
                act.op(lambda h, vg=vg, sm=sm, junk=junk: h.activation(out=junk[:], in_=vg[:, 512:1024], func=AF.Square,
                                                                       accum_out=sm[:, 1:2]), reads=[vtk], writes=[jtk, smk])
                dve.op(lambda h, sm=sm: h.tensor_tensor(out=sm[:, 2:3], in0=sm[:, 0:1], in1=sm[:, 1:2], op=ALU.add),
                       reads=[smk], writes=[smk])
                act.op(lambda h, sm=sm: h.activation(out=sm[:, 3:4], in_=sm[:, 2:3], func=AF.Sqrt, bias=eps_t[:], scale=1.0 / FW),
                       reads=[smk, tkC], writes=[smk])
                dve.op(lambda h, sm=sm: h.reciprocal(out=sm[:, 4:5], in_=sm[:, 3:4]), reads=[smk], writes=[smk])
                dve.op(lambda h, vg=vg, sm=sm, b=b: h.scalar_tensor_tensor(out=VCN[:, b, :], in0=vg[:], scalar=sm[:, 4:5],
                                                                             in1=gsgu[:], op0=ALU.mult, op1=ALU.mult),
                       reads=[vtk, smk, tkL], writes=[tkVCN])
            WF3 = wf[:].rearrange("p (k n) -> p k n", k=KC)
            for b in range(NB):
                gb = seg * NB + b
                ps, ps_tk = K.bank()
                mm_group(ps, ps_tk, [HT[:, k, b * 128:(b + 1) * 128] for k in range(KC)],
                         [WF3[:, k, :] for k in range(KC)], [tkWF, tkHT], cols=slice(0, H))
                sm, smk = SM.next()
                dve.op(lambda h, sm=sm, ps=ps: h.tensor_tensor(out=sm[:, 0:H], in0=ps[:, 0:H], in1=bfor[:, 0:H], op=ALU.add),
                       reads=[ps_tk, tkL], writes=[smk])
                act.op(lambda h, sm=sm: h.activation(out=sm[:, 0:H], in_=sm[:, 0:H], func=AF.Exp, scale=-1.0), reads=[smk], writes=[smk])
                act.op(lambda h, sm=sm: h.activation(out=sm[:, H:2 * H], in_=sm[:, 0:H], func=AF.Ln, bias=one_t[:], scale=1.0),
                       reads=[smk, tkC], writes=[smk])
                l1 = sm[:, H:2 * H]
                lo, lotk = lsum[ls_i], tkLS[ls_i]
                ln_, lntk = lsum[1 - ls_i], tkLS[1 - ls_i]
                ps2, ps2_tk = K.bank()
                mm_group(ps2, ps2_tk, [uneg[:], onesneg[:]], [l1, lo[:]], [smk, lotk, tkC], cols=slice(0, H))
                mm_group(ps2, ps2_tk, [halfneg[:], onesneg[:]], [l1, lo[:]], [smk, lotk, tkC], cols=slice(H, 2 * H))
                dve.op(lambda h, ps2=ps2, gb=gb: h.tensor_copy(out=CALL3[:, gb, :], in_=ps2[:, 0:H]), reads=[ps2_tk], writes=[tkCALL])
                dve.op(lambda h, ps2=ps2, gb=gb: h.tensor_copy(out=CREF3[:, gb, :], in_=ps2[:, H:2 * H]), reads=[ps2_tk], writes=[tkCALL])
                dve.op(lambda h, ln_=ln_, lo=lo, l1=l1: h.tensor_tensor(out=ln_[:], in0=lo[:], in1=l1, op=ALU.add),
                       reads=[smk, lotk], writes=[lntk])
                ls_i = 1 - ls_i
            K.barrier()
            psT, psT_tk = K.bank()
            mm_group(psT, psT_tk, [onesneg[:]], [lsum[ls_i][:]], [tkLS[ls_i], tkC], cols=slice(0, H))
            sm, smk = SM.next()
            dve.op(lambda h, sm=sm, psT=psT: h.tensor_copy(out=sm[:, 0:H], in_=psT[:, 0:H]), reads=[psT_tk], writes=[smk])
            for b in range(NB):
                dve.op(lambda h, b=b, sm=sm: h.tensor_tensor(out=CREL3[:, b, :], in0=CALL3[:, b, :], in1=sm[:, 0:H], op=ALU.subtract),
                       reads=[tkCALL, smk], writes=[tkCREL])
            sp.dma([(CRl[l], crel[:])], tkCREL, reads=[tkCREL], writes=[tkCRl[l]])
            pool.cc(PAIRS, CRl[l], CRg[l], tkCRg[l], reads=[tkCRl[l]], writes=[tkCRg[l]])
            sp.dma([(crelg[:], CRg[l][0:128, :])], tkCRELG, reads=[tkCRg[l]], writes=[tkCRELG])
            if debug:
                sp.dma([(dbg[(l * 4 + 0) * 128:(l * 4 + 1) * 128, :], crel[:])], tkCREL, reads=[tkCREL], writes=[tkDBG])
                sp.dma([(dbg[(l * 4 + 1) * 128:(l * 4 + 2) * 128, :], crelg[:])], tkCRELG, reads=[tkCRELG], writes=[tkDBG])

            for g in range(H):
                for tt in range(NT):
                    ps, ps_tk = K.bank()
                    for b4 in range(4):
                        b = tt * 4 + b4
                        pe.op(lambda h, ps=ps, b=b, b4=b4, g=g: h.matmul(ps[:, b4 * 128:(b4 + 1) * 128],
                                                                         lhsT=VCN[:, b, g * 128:(g + 1) * 128],
                                                                         rhs=WST3[:, g, :], start=True, stop=True),
                              reads=[tkVCN, tkL], writes=[ps_tk], inc=(b4 == 3))
                    tmp, ttk = T32.next()
                    for b4 in range(4):
                        dve.op(lambda h, tmp=tmp, ps=ps, b4=b4, g=g: h.tensor_tensor(
                            out=tmp[:, b4 * 128:(b4 + 1) * 128], in0=ps[:, b4 * 128:(b4 + 1) * 128], in1=BSP3[:, g, :], op=ALU.add),
                            reads=[ps_tk, tkL], writes=[ttk])
                    dve.op(lambda h, tmp=tmp, g=g, tt=tt: h.tensor_tensor(out=UT[:, g, tt * 512:(tt + 1) * 512], in0=tmp[:],
                                                                          in1=UT[:, g, tt * 512:(tt + 1) * 512], op=ALU.mult),
                           reads=[ttk, tkUT], writes=[tkUT])

            for hd in (range(H) if 'attn' not in SKIP else []):
                bi = hd % 2
                q_sb, k_sb, v_sb = ATQ[bi], ATK[bi], ATV[bi]
                hr = slice(hd * 128, (hd + 1) * 128)
                sp.dma([(q_sb, QT[hr, :])], tkATQ[bi], reads=[tkQT[hd]], writes=[tkATQ[bi]])
                h2 = slice((hd % 2) * 128, (hd % 2 + 1) * 128)
                sp.dma([(k_sb[:, TS:2 * TS], KTl[l][hd // 2][h2, :])], tkATK[bi], reads=[tkKT[hd]], writes=[tkATK[bi]])
                sp.dma([(v_sb[:, 16 + 4 * c:20 + 4 * c, :], Vl[l][c][:, hr].rearrange("(b p) d -> p b d", p=128)) for c in range(4)],
                       tkATV[bi], reads=tkVc, writes=[tkATV[bi]])
                sp.dma([(k_sb[:, 0:TS], KTg[l][hd // 2][h2, :])], tkATKp[bi], reads=[tkKTg[l][hd // 2]], writes=[tkATKp[bi]])
                sp.dma([(v_sb[:, 4 * c:4 * c + 4, :], Vg[l][c][0:512, hr].rearrange("(b p) d -> p b d", p=128)) for c in range(4)],
                       tkATVp[bi], reads=tkVg[l], writes=[tkATVp[bi]])
                if debug and hd == H - 1:
                    stg, stgk = T32.next()
                    dve.op(lambda h, stg=stg, k_sb=k_sb: h.tensor_copy(out=stg[:, 0:32], in_=k_sb[:, TS - 32:TS]), reads=[tkATKp[bi]], writes=[stgk])
                    dve.op(lambda h, stg=stg, k_sb=k_sb: h.tensor_copy(out=stg[:, 32:64], in_=k_sb[:, 2 * TS - 32:2 * TS]), reads=[tkATK[bi]], writes=[stgk])
                    dve.op(lambda h, stg=stg, v_sb=v_sb: h.tensor_copy(out=stg[:, 64:96], in_=v_sb[:, 15, 0:32]), reads=[tkATVp[bi]], writes=[stgk])
                    dve.op(lambda h, stg=stg, v_sb=v_sb: h.tensor_copy(out=stg[:, 96:128], in_=v_sb[:, 31, 0:32]), reads=[tkATV[bi]], writes=[stgk])
                    sp.dma([(dbg[(l * 4 + 2) * 128:(l * 4 + 3) * 128, 64:128], stg[:, 0:64]),
                            (dbg[(l * 4 + 3) * 128:(l * 4 + 4) * 128, 64:128], stg[:, 64:128])], stgk, reads=[stgk], writes=[tkDBG])
                for qt in range(NT):
                    gq0 = 16 + qt * 4
                    order = list(range(16, gq0 + 4)) + list(range(16))
                    nko = len(order)
                    psO, psO_tk = K.abank()
                    psD, psD_tk = K.abank()

                    def emit_od(st):
                        (idx, kb, cols, p, ptk, vtk_) = st
                        pe.op(lambda h, psO=psO, cols=cols, kb=kb, p=p, v_sb=v_sb, idx=idx, nko=nko: h.matmul(
                            psO[:, cols], lhsT=v_sb[:, kb, :], rhs=p[:, cols], start=(idx == 0), stop=(idx == nko - 1)),
                            reads=[vtk_, ptk], writes=[psO_tk], inc=False)
                        pe.op(lambda h, psD=psD, cols=cols, p=p, idx=idx, nko=nko: h.matmul(
                            psD[:, cols], lhsT=ones_bf[:], rhs=p[:, cols], start=(idx == 0), stop=(idx == nko - 1)),
                            reads=[ptk, tkC], writes=[psD_tk])

                    pend = []
                    for idx, kb in enumerate(order):
                        own = kb >= 16
                        ktk = tkATK[bi] if own else tkATKp[bi]
                        vtk_ = tkATV[bi] if own else tkATVp[bi]
                        j0 = max(0, kb - gq0)
                        cols = slice(j0 * 128, 512)
                        bb, bbk = biasb.next()
                        if own:
                            dve.op(lambda h, bb=bb, kb=kb, qt=qt, hd=hd: h.tensor_scalar(
                                out=bb[:, 0:4], in0=CREF3[:, qt * 4:qt * 4 + 4, hd], scalar1=CALL3[:, kb - 16, hd:hd + 1],
                                scalar2=None, op0=ALU.subtract), reads=[tkCALL], writes=[bbk])
                        else:
                            dve.op(lambda h, bb=bb, kb=kb, qt=qt, hd=hd: h.tensor_scalar(
                                out=bb[:, 0:4], in0=CREF3[:, qt * 4:qt * 4 + 4, hd], scalar1=CRELG3[:, kb, hd:hd + 1],
                                scalar2=pmask[:, 0:1], op0=ALU.subtract, op1=ALU.add), reads=[tkCALL, tkCRELG, tkC], writes=[bbk])
                        psS, psS_tk = K.bank()
                        pe.op(lambda h, psS=psS, cols=cols, kb=kb, qt=qt, j0=j0, k_sb=k_sb, q_sb=q_sb: h.matmul(
                            psS[:, cols], lhsT=k_sb[:, kb * 128:(kb + 1) * 128],
                            rhs=q_sb[:, qt * 512 + j0 * 128:(qt + 1) * 512], start=True, stop=True),
                            reads=[ktk, tkATQ[bi]], writes=[psS_tk])
                        p, ptk = PT.next()
                        for j in range(j0, 4):
                            act.op(lambda h, p=p, psS=psS, j=j, bb=bb: h.activation(
                                out=p[:, j * 128:(j + 1) * 128], in_=psS[:, j * 128:(j + 1) * 128], func=AF.Exp,
                                bias=bb[:, j:j + 1], scale=1.0), reads=[psS_tk, bbk], writes=[ptk])
                        if kb >= gq0:
                            dve.op(lambda h, p=p, j0=j0: h.tensor_tensor(out=p[:, j0 * 128:(j0 + 1) * 128],
                                                                          in0=p[:, j0 * 128:(j0 + 1) * 128], in1=mask_bf[:], op=ALU.mult),
                                   reads=[ptk, tkC], writes=[ptk])
                        pend.append((idx, kb, cols, p, ptk, vtk_))
                        if len(pend) > 2:
                            emit_od(pend.pop(0))
                    while pend:
                        emit_od(pend.pop(0))
                    rd, rdk = T32.next()
                    dve.op(lambda h, rd=rd, psD=psD: h.reciprocal(out=rd[:], in_=psD[:]), reads=[psD_tk], writes=[rdk])
                    dve.op(lambda h, rd=rd, psO=psO, hd=hd, qt=qt: h.tensor_tensor(
                        out=YA[:, hd, qt * 512:(qt + 1) * 512], in0=psO[:], in1=rd[:], op=ALU.mult),
                        reads=[psO_tk, rdk], writes=[tkYA])
            K.barrier()

            Wa, Wb = w_ba[l], w_bb[l]
            for g0 in (range(0, KC, 4) if 'merge' not in SKIP else []):
                slot, stk = wload([(lambda s: wview(s, 16, 512)[:, 0:8, :], wsrc(Wa, 0, 8, g0 * 128, 512)),
                                   (lambda s: wview(s, 16, 512)[:, 8:16, :], wsrc(Wb, 0, 8, g0 * 128, 512))])
                wv = wview(slot, 16, 512)
                for j in range(4):
                    n = g0 + j
                    for tt in range(NT):
                        psA, psA_tk = K.bank()
                        psB, psB_tk = K.bank()
                        mm_group(psA, psA_tk, [wv[:, k, j * 128:(j + 1) * 128] for k in range(8)],
                                 [YA[:, k, tt * 512:(tt + 1) * 512] for k in range(8)], [stk, tkYA])
                        mm_group(psB, psB_tk, [wv[:, 8 + k, j * 128:(j + 1) * 128] for k in range(8)],
                                 [YB[:, k, tt * 512:(tt + 1) * 512] for k in range(8)], [stk, tkUT])
                        ga, gatk = GTL.next()
                        gb_, gbtk = GTL.next()
                        sp.dma([(ga[:], GA[n * 128:(n + 1) * 128, tt * 512:(tt + 1) * 512])], gatk, reads=[tkGA[n]], writes=[gatk])
                        sp.dma([(gb_[:], GB[n * 128:(n + 1) * 128, tt * 512:(tt + 1) * 512])], gbtk, reads=[tkGB[n]], writes=[gbtk])
                        t1, t1k = T32.next()
                        t2, t2k = T32.next()
                        dve.op(lambda h, t1=t1, psA=psA, ga=ga: h.tensor_tensor(out=t1[:], in0=psA[:], in1=ga[:], op=ALU.mult),
                               reads=[psA_tk, gatk], writes=[t1k])
                        dve.op(lambda h, t2=t2, psB=psB, gb_=gb_: h.tensor_tensor(out=t2[:], in0=psB[:], in1=gb_[:], op=ALU.mult),
                               reads=[psB_tk, gbtk], writes=[t2k])
                        dve.op(lambda h, t1=t1, t2=t2, n=n, tt=tt: h.tensor_tensor(
                            out=MERGED[:, n, tt * 512:(tt + 1) * 512], in0=t1[:], in1=t2[:], op=ALU.add),
                            reads=[t1k, t2k], writes=[tkHT])

            gemm_fm(w_out[l], 0, KC, KC, hin, [tkHT],
                    lambda n, tt, ps, ps_tk: resid_epilogue(n, seg * NT + tt, ps, ps_tk), seg)

            if debug:
                tkd = Tk("dbgx%da" % l)
                sp.dma([(dbgx[(2 * l) * D:(2 * l + 1) * D, :], X[:, 0:128])], tkd, reads=[tkX[n][0] for n in range(KC)])
            XTL3 = xtl_sb[:].rearrange("p (k t) -> p k t", k=KC)
            XTG3 = xtg_sb[:].rearrange("p (k t) -> p k t", k=KC)
            sp.dma([(XTL3, X[:, TS - 2:TS].rearrange("(k p) t -> p k t", p=128))], tkXTL,
                   reads=[tkX[n][NT - 1] for n in range(KC)], writes=[tkXTL])
            sp.dma([(XTl[l], xtl_sb[:])], tkXTL, reads=[tkXTL], writes=[tkXTl[l]])
            pool.cc(PAIRS, XTl[l], XTg[l], tkXTg[l], reads=[tkXTl[l]], writes=[tkXTg[l]])
            sp.dma([(xtg_sb[:], XTg[l][0:128, :])], tkXTGs, reads=[tkXTg[l]], writes=[tkXTGs])
            if debug:
                sp.dma([(dbg[(l * 4 + 2) * 128:(l * 4 + 3) * 128, 0:32], xtl_sb[:])], tkXTL, reads=[tkXTL], writes=[tkDBG])
                sp.dma([(dbg[(l * 4 + 3) * 128:(l * 4 + 4) * 128, 0:32], xtg_sb[:])], tkXTGs, reads=[tkXTGs], writes=[tkDBG])
            norm_stage(seg, gffn, 0)
            sqx, sqxk = OT.next()
            act.op(lambda h, sqx=sqx: h.activation(out=sqx[:, 0:32], in_=xtg_sb[:], func=AF.Square), reads=[tkXTGs], writes=[sqxk])
            psX, psX_tk = K.bank()
            mm_group(psX, psX_tk, [ones_bf[:]] * KC, [sqx[:, 2 * k:2 * k + 2] for k in range(KC)], [sqxk, tkC], cols=slice(0, 2))
            smx, smxk = SM.next()
            act.op(lambda h, smx=smx, psX=psX: h.activation(out=smx[:, 0:2], in_=psX[:, 0:2], func=AF.Sqrt, bias=eps_t[:], scale=1.0 / D),
                   reads=[psX_tk, tkC], writes=[smxk])
            dve.op(lambda h, smx=smx: h.reciprocal(out=smx[:, 2:4], in_=smx[:, 0:2]), reads=[smxk], writes=[smxk])
            for k in range(KC):
                dve.op(lambda h, k=k, smx=smx: h.scalar_tensor_tensor(out=HX3[:, k, :], in0=XTG3[:, k, :], scalar=gffn[:, k:k + 1],
                                                             in1=smx[:, 2:4], op0=ALU.mult, op1=ALU.mult),
                       reads=[tkXTGs, smxk, tkL], writes=[tkHX])
            K.barrier()

            Wu = w_up[l]
            CW = convw[:].rearrange("p (t f) -> p t f", t=3)
            CB = convb[:]
            for g0 in (range(0, FC, 2) if 'ffnup' not in SKIP else []):
                slot, stk = wload([(lambda s: wview(s, KC, 512)[:, :, 0:256], wsrc(Wu, 0, KC, g0 * 128, 256)),
                                   (lambda s: wview(s, KC, 512)[:, :, 256:512], wsrc(Wu, 0, KC, DFF + g0 * 128, 256))])
                wv = wview(slot, KC, 512)
                for j in range(2):
                    n = g0 + j
                    asb, atk = ASB[n % 2], tkASB[n % 2]
                    for tt in range(NT):
                        psA, psA_tk = K.bank()
                        psB, psB_tk = K.bank()
                        mm_group(psA, psA_tk, [wv[:, k, j * 128:(j + 1) * 128] for k in range(KC)],
                                 [hin(k, tt) for k in range(KC)], [stk, tkHT])
                        mm_group(psB, psB_tk, [wv[:, k, 256 + j * 128:256 + (j + 1) * 128] for k in range(KC)],
                                 [hin(k, tt) for k in range(KC)], [stk, tkHT])
                        c0 = tt * 512
                        if tt == 0:
                            psH, psH_tk = K.abank()
                            mm_group(psH, psH_tk, [wv[:, k, j * 128:(j + 1) * 128] for k in range(KC)],
                                     [HX3[:, k, :] for k in range(KC)], [stk, tkHX], cols=slice(0, 2))
                            dve.op(lambda h, asb=asb, psH=psH: h.tensor_scalar(out=asb[:, 0:2], in0=psH[:, 0:2], scalar1=pflag[:, 0:1],
                                                                               scalar2=None, op0=ALU.mult),
                                   reads=[psH_tk, tkC], writes=[atk])
                        act.op(lambda h, asb=asb, psA=psA, c0=c0: h.activation(out=asb[:, 2 + c0:2 + c0 + 512], in_=psA[:], func=AF.Copy),
                               reads=[psA_tk], writes=[atk])
                        acc, acck = ACC[tt % 2], tkACC[tt % 2]
                        dve.op(lambda h, acc=acc, asb=asb, c0=c0, n=n: h.tensor_scalar(
                            out=acc, in0=asb[:, 2 + c0:2 + c0 + 512], scalar1=CW[:, 2, n:n + 1], scalar2=CB[:, n:n + 1],
                            op0=ALU.mult, op1=ALU.add), reads=[atk, tkL], writes=[acck])
                        dve.op(lambda h, acc=acc, asb=asb, c0=c0, n=n: h.scalar_tensor_tensor(
                            out=acc, in0=asb[:, 1 + c0:1 + c0 + 512], scalar=CW[:, 1, n:n + 1], in1=acc,
                            op0=ALU.mult, op1=ALU.add), reads=[atk, tkL], writes=[acck])
                        dve.op(lambda h, acc=acc, asb=asb, c0=c0, n=n: h.scalar_tensor_tensor(
                            out=acc, in0=asb[:, c0:c0 + 512], scalar=CW[:, 0, n:n + 1], in1=acc,
                            op0=ALU.mult, op1=ALU.add), reads=[atk, tkL], writes=[acck])
                        gg, ggk = GG[tt % 2], tkGG[tt % 2]
                        act.op(lambda h, gg=gg, acc=acc: h.activation(out=gg, in_=acc, func=AF.Gelu_apprx_tanh),
                               reads=[acck], writes=[ggk])
                        ot, otk = OT.next()
                        dve.op(lambda h, ot=ot, gg=gg, psB=psB: h.tensor_tensor(out=ot[:], in0=psB[:], in1=gg, op=ALU.mult),
                               reads=[psB_tk, ggk], writes=[otk])
                        sp.dma([(GT[n * 128:(n + 1) * 128, c0:c0 + 512], ot[:])], otk, reads=[otk], writes=[tkGT[n]])
            K.barrier()

            Wd = w_down[l]
            for half in (range(2) if 'ffndn' not in SKIP else []):
                sp.dma([(GTH, GT[half * 2816:(half + 1) * 2816, :].rearrange("(k p) t -> p k t", p=128))], tkGTH,
                       reads=tkGT[half * 22:(half + 1) * 22], writes=[tkGTH])
                gin = lambda k, tt: GTH[:, k, tt * 512:(tt + 1) * 512]
                gemm_fm(Wd, 0, KC, 22, gin, [tkGTH],
                        lambda n, tt, ps, ps_tk: resid_epilogue(n, seg * NT + tt, ps, ps_tk), seg, group=2, r0=half * 2816)
            K.barrier()

        if debug:
            tkd = Tk("dbgx%db" % l)
            sp.dma([(dbgx[(2 * l + 1) * D:(2 * l + 2) * D, :], X[:, 0:128])], tkd, reads=[tkX[n][0] for n in range(KC)])
    for seg in range(NSEG):
        if final_norm:
            norm_stage(seg, gfin, 0, to_out=True)
        else:
            pass
    if not final_norm:
        tko = Tk("ocopy")
        sp.dma([(outT[i * 512:(i + 1) * 512, :], X[i * 512:(i + 1) * 512, :]) for i in range(4)], tko,
               reads=[t for row in tkX for t in row])
    return K.finish()


def _pm(v, nchunk):
    return np.ascontiguousarray(v.reshape(nchunk, 128).T)


def host_layout(L, g_mix, b_forget, g_sgu, w_spatial, b_spatial, g_ffn, conv_w, conv_b, g_final):
    f = np.float32
    m = {}
    m["gmix_h"] = np.ascontiguousarray(np.concatenate([_pm(g_mix[l], KC) for l in range(L)], 1), f)
    m["gffn_h"] = np.ascontiguousarray(np.concatenate([_pm(g_ffn[l], KC) for l in range(L)], 1), f)
    m["gfin_h"] = np.ascontiguousarray(_pm(g_final, KC), f)
    m["convw_h"] = np.ascontiguousarray(np.concatenate([_pm(conv_w[l, t], FC) for l in range(L) for t in range(3)], 1), f)
    m["convb_h"] = np.ascontiguousarray(np.concatenate([_pm(conv_b[l], FC) for l in range(L)], 1), f)
    m["bfor_h"] = np.ascontiguousarray(np.broadcast_to(b_forget[:L].reshape(1, L * H), (128, L * H)), f)
    m["gsgu_h"] = np.ascontiguousarray(np.broadcast_to(g_sgu[:L, None, :], (L, 128, FW)), f)
    m["bsp_h"] = np.ascontiguousarray(np.broadcast_to(b_spatial[:L].reshape(L, 1, H * 128), (L, 128, H * 128)), f)
    m["wsT_h"] = np.ascontiguousarray(np.transpose(w_spatial[:L], (0, 3, 1, 2)).reshape(L, 128, H * 128), f)
    return m


_NC_CACHE = {}
DEBUG = False
SKIP = set()


def kernel(x, g_mix, w_in, b_forget, g_sgu, w_spatial, b_spatial, w_branch_a, w_branch_b, w_out,
           g_ffn, w_up, conv_w, conv_b, w_down, g_final):
    x = np.asarray(x)
    B, S, _ = x.shape
    assert S == 2 * TS and 2 * B <= 8
    L = int(np.asarray(w_in).shape[0])
    args = [np.asarray(a, dtype=np.float32) for a in (g_mix, b_forget, g_sgu, w_spatial, b_spatial, g_ffn, conv_w, conv_b, g_final)]
    base = host_layout(L, *args)
    for name, a in (("w_in", w_in), ("w_branch_a", w_branch_a), ("w_branch_b", w_branch_b), ("w_out", w_out),
                    ("w_up", w_up), ("w_down", w_down)):
        base[name] = np.ascontiguousarray(np.asarray(a, dtype=np.float32))
    key = (L,)
    if key not in _NC_CACHE:
        _NC_CACHE[key] = build(L, TS, debug=DEBUG)
    nc = _NC_CACHE[key]
    ncore = 8
    in_maps = []
    for c in range(ncore):
        b, r = (c // 2) % B, c % 2
        m = dict(base)
        m["xT"] = np.ascontiguousarray(x[b, r * TS:(r + 1) * TS].T.astype(np.float32))
        pm = np.zeros((128, 16), np.float32)
        pm[:, 0] = -30000.0 if r == 0 else 0.0
        pm[:, 1] = 0.0 if r == 0 else 1.0
        m["pm"] = pm
        in_maps.append(m)
    res = run_bass_kernel_spmd(nc, in_maps, core_ids=list(range(ncore)))
    out = np.empty((B, S, D), np.float32)
    for c in range(2 * B):
        b, r = c // 2, c % 2
        out[b, r * TS:(r + 1) * TS] = res.results[c]["outT"].T
    if DEBUG:
        kernel.dbg = [res.results[c]["dbg"] for c in range(ncore)]
        kernel.dbgx = [res.results[c]["dbgx"] for c in range(ncore)]
    return out
```

```python
import contextlib
import numpy as np
import concourse.bass as bass
import concourse.mybir as mybir
from concourse.bass_utils import run_bass_kernel_spmd

F32 = mybir.dt.float32
BF16 = mybir.dt.bfloat16
AF = mybir.ActivationFunctionType
ALU = mybir.AluOpType

D = 2048
SEQ = 4096
DEPTH = 4
H = 8
DH = 128
FW = 1024
DFF = 5632
IN_W = 3 * FW + H + 2 * FW + 2 * D
KC = D // 128
FC = DFF // 128
TS = 2048
EPS = 1e-6
OQ, OK_, OV, OF, OU, OVG, OGA, OGB = 0, 1024, 2048, 3072, 3080, 4104, 5128, 7176

SAME_ENG_SYNC = True
SEM_ROT = 30000


class Tk:
    __slots__ = ("name", "w", "rs", "dsem", "dcnt", "dkey", "depoch")

    def __init__(self, name=""):
        self.name = name
        self.w = None
        self.rs = {}
        self.dsem = None
        self.dcnt = 0
        self.dkey = None
        self.depoch = 0


class Eng:
    def __init__(self, K, name, handle, kind):
        self.K = K
        self.name = name
        self.h = handle
        self.kind = kind
        self.epoch = 0
        self.sem = K.new_sem("e_%s_0" % name)
        self.key = "%s#0" % name
        self.n = 0
        self.seen = {}
        self.prog = []
        self.pending = False

    def _rot(self):
        if self.n >= SEM_ROT and not self.pending:
            self.epoch += 1
            self.sem = self.K.new_sem("e_%s_%d" % (self.name, self.epoch))
            self.key = "%s#%d" % (self.name, self.epoch)
            self.n = 0

    def _wait(self, ev, war=False):
        if ev is None:
            return
        sem, val, key, owner = ev
        if owner is self:
            if war or self.kind in ("pe", "sp") or not SAME_ENG_SYNC:
                return
        if self.seen.get(key, 0) >= val:
            return
        self.seen[key] = val
        self.prog.append(lambda h, s=sem, v=val: h.wait_ge(s, v))

    def deps(self, reads, writes):
        for t in reads:
            self._wait(t.w)
        for t in writes:
            self._wait(t.w, war=True)
            for r in t.rs.values():
                self._wait(r, war=True)

    def op(self, fn, reads=(), writes=(), inc=True):
        self.deps(reads, writes)
        self._rot()
        ev = (self.sem, self.n + 1, self.key, self)
        self.pending = not inc
        if inc:
            self.n += 1
            sem = self.sem
            self.prog.append(lambda h, f=fn, s=sem: f(h).then_inc(s, 1))
        else:
            self.prog.append(lambda h, f=fn: f(h))
        for t in reads:
            t.rs[self.key] = ev
        for t in writes:
            t.w = ev
            t.rs = {}
        return ev

    def dma(self, pairs, slot, reads=(), writes=()):
        self.deps(reads, writes)
        K = self.K
        if slot.dsem is None or slot.dcnt + 16 * len(pairs) > SEM_ROT:
            slot.depoch += 1
            slot.dsem = K.new_sem("d_%s_%d" % (slot.name, slot.depoch))
            slot.dkey = "d_%s#%d#%d" % (slot.name, id(slot), slot.depoch)
            slot.dcnt = 0
        sem = slot.dsem
        for (o, i) in pairs:
            slot.dcnt += 16
            self.prog.append(lambda h, o=o, i=i, s=sem: h.dma_start(out=o, in_=i).then_inc(s, 16))
        ev = (sem, slot.dcnt, slot.dkey, None)
        K.dirty[slot.dkey] = ev
        for t in reads:
            t.rs[slot.dkey] = ev
        for t in writes:
            t.w = ev
            t.rs = {}
        return ev


def _eng_cc(self, groups, in_ap, out_ap, slot, reads=(), writes=()):
    self.deps(reads, writes)
    K = self.K
    if slot.dsem is None:
        slot.depoch += 1
        slot.dsem = K.new_sem("c_%s_%d" % (slot.name, slot.depoch))
        slot.dkey = "c_%s#%d#%d" % (slot.name, id(slot), slot.depoch)
        slot.dcnt = 0
    sem = slot.dsem
    slot.dcnt += 1
    self.prog.append(lambda h, s=sem: h.collective_compute(
        "AllGather", ALU.bypass, replica_groups=groups, ins=[in_ap.opt()], outs=[out_ap.opt()]).then_inc(s, 1))
    ev = (sem, slot.dcnt, slot.dkey, None)
    K.dirty[slot.dkey] = ev
    for t in reads:
        t.rs[slot.dkey] = ev
    for t in writes:
        t.w = ev
        t.rs = {}
    return ev


Eng.cc = _eng_cc
PAIRS = [[0, 1], [2, 3], [4, 5], [6, 7]]


class Kern:
    def __init__(self):
        self.nc = bass.Bass("TRN2", target_bir_lowering=False)
        self.stack = contextlib.ExitStack()
        self.nsem = 0
        self.dirty = {}
        self.pe = Eng(self, "pe", self.nc.tensor, "pe")
        self.act = Eng(self, "act", self.nc.scalar, "act")
        self.dve = Eng(self, "dve", self.nc.vector, "dve")
        self.pool = Eng(self, "pool", self.nc.gpsimd, "pool")
        self.sp = Eng(self, "sp", self.nc.sync, "sp")
        self.engs = [self.pe, self.act, self.dve, self.pool, self.sp]
        self.banks = []
        self.bank_i = 0
        self.abank_i = 0

    def new_sem(self, name):
        self.nsem += 1
        return self.stack.enter_context(self.nc.semaphore(name))

    def dram(self, name, shape, dtype, kind="Internal"):
        return self.nc.dram_tensor(name, list(shape), dtype, kind=kind).ap()

    def sbuf(self, name, shape, dtype):
        return self.stack.enter_context(self.nc.sbuf_tensor(name, list(shape), dtype))

    def make_banks(self):
        for i in range(8):
            t = self.stack.enter_context(self.nc.psum_tensor("bank%d" % i, [128, 512], F32))
            self.banks.append((t, Tk("bank%d" % i)))

    def bank(self):
        b = self.banks[self.bank_i % 4]
        self.bank_i += 1
        return b

    def abank(self):
        b = self.banks[4 + self.abank_i % 4]
        self.abank_i += 1
        return b

    def barrier(self):
        evs = [(e.sem, e.n, e.key, None) for e in self.engs if e.n > 0]
        evs += list(self.dirty.values())
        self.dirty = {}
        for e in self.engs:
            for ev in evs:
                if ev[2] == e.key:
                    continue
                e._wait(ev)

    def finish(self):
        self.barrier()
        nc = self.nc
        with nc.Block() as block:
            @block.tensor
            def _(e):
                for f in self.pe.prog:
                    f(e)

            @block.scalar
            def _(e):
                for f in self.act.prog:
                    f(e)

            @block.vector
            def _(e):
                for f in self.dve.prog:
                    f(e)

            @block.gpsimd
            def _(e):
                for f in self.pool.prog:
                    f(e)

            @block.sync
            def _(e):
                for f in self.sp.prog:
                    f(e)
        self.stack.close()
        return nc


class Rot:
    def __init__(self, K, name, n, shape, dtype):
        self.items = []
        for i in range(n):
            self.items.append((K.sbuf("%s%d" % (name, i), shape, dtype), Tk("%s%d" % (name, i))))
        self.i = 0

    def next(self):
        it = self.items[self.i % len(self.items)]
        self.i += 1
        return it


def build(L, S, final_norm=True, debug=False):
    K = Kern()
    nc = K.nc
    pe, act, dve, pool, sp = K.pe, K.act, K.dve, K.pool, K.sp
    NSEG = S // TS
    NBLK = S // 128
    NT = TS // 512
    NB = TS // 128

    EI = "ExternalInput"
    xT_in = K.dram("xT", [D, S], F32, EI)
    w_in = K.dram("w_in", [L, D, IN_W], F32, EI)
    w_ba = K.dram("w_branch_a", [L, FW, D], F32, EI)
    w_bb = K.dram("w_branch_b", [L, FW, D], F32, EI)
    w_out = K.dram("w_out", [L, D, D], F32, EI)
    w_up = K.dram("w_up", [L, D, 2 * DFF], F32, EI)
    w_down = K.dram("w_down", [L, DFF, D], F32, EI)
    gmix_h = K.dram("gmix_h", [128, L * KC], F32, EI)
    gffn_h = K.dram("gffn_h", [128, L * KC], F32, EI)
    gfin_h = K.dram("gfin_h", [128, KC], F32, EI)
    convw_h = K.dram("convw_h", [128, L * 3 * FC], F32, EI)
    convb_h = K.dram("convb_h", [128, L * FC], F32, EI)
    bfor_h = K.dram("bfor_h", [128, L * H], F32, EI)
    gsgu_h = K.dram("gsgu_h", [L, 128, FW], F32, EI)
    bsp_h = K.dram("bsp_h", [L, 128, H * 128], F32, EI)
    wsT_h = K.dram("wsT_h", [L, 128, H * 128], F32, EI)
    outT = K.dram("outT", [D, S], F32, "ExternalOutput")
    dbg = K.dram("dbg", [L * 4 * 128, 128], F32, "ExternalOutput") if debug else None
    tkDBG = Tk("dbg")
    dbgx = K.dram("dbgx", [L * 2 * D, 128], F32, "ExternalOutput") if debug else None

    X = K.dram("X", [D, S], F32)
    QT = K.dram("QT", [FW, TS], BF16)
    pm_h = K.dram("pm", [128, 16], F32, EI)
    NBUF = 1
    KTl = [[K.dram("KTl%d_%d" % (l, c), [256, TS], BF16) for c in range(4)] for l in range(NBUF)] * L
    KTg = [[K.dram("KTg%d_%d" % (l, c), [512, TS], BF16) for c in range(4)] for l in range(NBUF)] * L
    Vl = [[K.dram("Vl%d_%d" % (l, c), [512, FW], BF16) for c in range(4)] for l in range(NBUF)] * L
    Vg = [[K.dram("Vg%d_%d" % (l, c), [1024, FW], BF16) for c in range(4)] for l in range(NBUF)] * L
    CRl = [K.dram("CRl%d" % l, [128, 128], F32) for l in range(NBUF)] * L
    CRg = [K.dram("CRg%d" % l, [256, 128], F32) for l in range(NBUF)] * L
    XTl = [K.dram("XTl%d" % l, [128, 32], F32) for l in range(NBUF)] * L
    XTg = [K.dram("XTg%d" % l, [256, 32], F32) for l in range(NBUF)] * L
    GA = K.dram("GA", [D, TS], BF16)
    GB = K.dram("GB", [D, TS], BF16)
    GT = K.dram("GT", [DFF, TS], BF16)
    tkX = [[Tk("X%d_%d" % (n, t)) for t in range(S // 512)] for n in range(KC)]
    tkQT = [Tk("QT%d" % n) for n in range(H)]
    tkKT = [Tk("KT%d" % n) for n in range(H)]
    tkV = Tk("V")
    tkKTg = [[Tk("KTg%d_%d" % (l, c)) for c in range(4)] for l in range(NBUF)] * L
    tkVg = [[Tk("Vg%d_%d" % (l, c)) for c in range(4)] for l in range(NBUF)] * L
    tkVc = [Tk("Vc%d" % c) for c in range(4)]
    tkCRl = [Tk("CRl%d" % l) for l in range(NBUF)] * L
    tkCRg = [Tk("CRg%d" % l) for l in range(NBUF)] * L
    tkXTl = [Tk("XTl%d" % l) for l in range(NBUF)] * L
    tkXTg = [Tk("XTg%d" % l) for l in range(NBUF)] * L
    tkGA = [Tk("GA%d" % n) for n in range(KC)]
    tkGB = [Tk("GB%d" % n) for n in range(KC)]
    tkGT = [Tk("GT%d" % n) for n in range(FC)]

    K.make_banks()
    ARENA = K.sbuf("arena", [128, 65536], BF16)
    tkA = Tk("arena")

    def a_bf(off, n):
        return ARENA[:, off // 2: off // 2 + n]

    def a_f32(off, n):
        return ARENA[:, off // 2: off // 2 + 2 * n].bitcast(F32)

    R2 = 65536
    HT = a_bf(0, KC * TS).rearrange("p (k t) -> p k t", k=KC)
    UT = a_bf(R2, H * TS).rearrange("p (k t) -> p k t", k=H)
    VCN = a_bf(R2 + 32768, NB * FW).rearrange("p (b f) -> p b f", b=NB)
    YA = a_bf(R2 + 32768, H * TS).rearrange("p (k t) -> p k t", k=H)
    YB = UT
    MERGED = HT
    XS = [a_f32(R2 + i * 16384, KC * 256).rearrange("p (k t) -> p k t", k=KC) for i in range(2)]
    SQ = a_bf(R2 + 32768, KC * 256).rearrange("p (k t) -> p k t", k=KC)
    tkXS = [Tk("XS0"), Tk("XS1")]
    tkSQ = Tk("SQ")
    tkHT = Tk("HT")
    tkUT = Tk("UT")
    tkVCN = Tk("VCN")
    tkYA = tkVCN
    ATQ = [a_bf(i * 4096, TS) for i in range(2)]
    ATK = [a_bf(8192 + i * 8192, 2 * TS) for i in range(2)]
    ATV = [a_bf(24576 + i * 8192, 32 * 128).rearrange("p (b d) -> p b d", b=32) for i in range(2)]
    tkATQ = [Tk("ATQ0"), Tk("ATQ1")]
    tkATK = [Tk("ATK0"), Tk("ATK1")]
    tkATV = [Tk("ATV0"), Tk("ATV1")]
    tkATKp = [Tk("ATKp0"), Tk("ATKp1")]
    tkATVp = [Tk("ATVp0"), Tk("ATVp1")]
    FU = 90112
    ASB = [a_f32(FU + i * 8208, 2 + TS) for i in range(2)]
    tkASB = [Tk("ASB0"), Tk("ASB1")]
    ACC = [a_f32(FU + 16416 + i * 2048, 512) for i in range(2)]
    tkACC = [Tk("ACC0"), Tk("ACC1")]
    GG = [a_f32(FU + 20512 + i * 2048, 512) for i in range(2)]
    tkGG = [Tk("GG0"), Tk("GG1")]
    GTH = a_bf(0, 22 * TS).rearrange("p (k t) -> p k t", k=22)
    tkGTH = Tk("GTH")

    WS = Rot(K, "wslot", 2, [128, 8192], BF16)
    OT = Rot(K, "ot", 6, [128, 512], BF16)
    GTL = OT
    PT = Rot(K, "pt", 3, [128, 512], BF16)
    T32 = Rot(K, "t32", 6, [128, 512], F32)
    XT = T32
    VGF = Rot(K, "vgf", 2, [128, FW], F32)
    SM = Rot(K, "sm", 8, [128, 16], F32)

    ones_bf = K.sbuf("ones_bf", [128, 128], BF16)
    zeros_bf = K.sbuf("zeros_bf", [128, 128], BF16)
    S4 = Rot(K, "s4t", 2, [128, 512], BF16)
    mask_bf = K.sbuf("mask_bf", [128, 128], BF16)
    uneg = K.sbuf("uneg", [128, 128], F32)
    onesneg = K.sbuf("onesneg", [128, 128], F32)
    halfneg = K.sbuf("halfneg", [128, 128], F32)
    eps_t = K.sbuf("eps_t", [128, 1], F32)
    one_t = K.sbuf("one_t", [128, 1], F32)
    gmix = K.sbuf("gmix", [128, KC], F32)
    gffn = K.sbuf("gffn", [128, KC], F32)
    gfin = K.sbuf("gfin", [128, KC], F32)
    convw = K.sbuf("convw", [128, 3 * FC], F32)
    convb = K.sbuf("convb", [128, FC], F32)
    bfor = K.sbuf("bfor", [128, H], F32)
    gsgu = K.sbuf("gsgu", [128, FW], F32)
    bsp = K.sbuf("bsp", [128, H * 128], F32)
    wst_f = VGF.items[0][0]
    wst = K.sbuf("wst", [128, H * 128], BF16)
    wf = K.sbuf("wf", [128, KC * H], BF16)
    call = K.sbuf("call", [128, NBLK * H], F32)
    cref = K.sbuf("cref", [128, NBLK * H], F32)
    lsum = [K.sbuf("lsum%d" % i, [128, H], F32) for i in range(2)]
    crel = K.sbuf("crel", [128, NBLK * H], F32)
    crelg = K.sbuf("crelg", [128, NBLK * H], F32)
    pm_sb = K.sbuf("pm_sb", [128, 16], F32)
    pmask = pm_sb[:, 0:1]
    pflag = pm_sb[:, 1:2]
    xtl_sb = K.sbuf("xtl_sb", [128, 32], F32)
    xtg_sb = K.sbuf("xtg_sb", [128, 32], F32)
    hx = K.sbuf("hx", [128, 32], BF16)
    tkCREL = Tk("crel")
    tkCRELG = Tk("crelg")
    tkXTL = Tk("xtl")
    tkXTGs = Tk("xtg")
    tkHX = Tk("hx")
    biasb = Rot(K, "biasb", 4, [128, 16], F32)
    tkC = Tk("consts")
    tkL = Tk("layerc")
    tkWF = Tk("wf")
    tkCALL = Tk("call")
    tkLS = [Tk("ls0"), Tk("ls1")]

    CALL3 = call[:].rearrange("p (b h) -> p b h", h=H)
    CREF3 = cref[:].rearrange("p (b h) -> p b h", h=H)
    CREL3 = crel[:].rearrange("p (b h) -> p b h", h=H)
    CRELG3 = crelg[:].rearrange("p (b h) -> p b h", h=H)
    HX3 = hx[:].rearrange("p (k t) -> p k t", k=KC)

    pool.op(lambda h: h.memset(ones_bf[:], 1.0), writes=[tkC])
    pool.op(lambda h: h.memset(zeros_bf[:], 0.0), writes=[tkC])
    pool.op(lambda h: h.memset(mask_bf[:], 1.0), writes=[tkC])
    pool.op(lambda h: h.memset(uneg[:], -1.0), writes=[tkC])
    pool.op(lambda h: h.memset(onesneg[:], -1.0), writes=[tkC])
    pool.op(lambda h: h.memset(halfneg[:], -1.0), writes=[tkC])
    pool.op(lambda h: h.memset(eps_t[:], EPS), writes=[tkC])
    pool.op(lambda h: h.memset(one_t[:], 1.0), writes=[tkC])
    pool.op(lambda h: h.affine_select(out=mask_bf[:], in_=mask_bf[:], pattern=[[1, 128]], compare_op=ALU.is_ge,
                                      fill=0.0, base=0, channel_multiplier=-1), writes=[tkC])
    pool.op(lambda h: h.affine_select(out=uneg[:], in_=uneg[:], pattern=[[1, 128]], compare_op=ALU.is_ge,
                                      fill=0.0, base=0, channel_multiplier=-1), writes=[tkC])
    pool.op(lambda h: h.affine_select(out=halfneg[:], in_=halfneg[:], pattern=[[0, 128]], compare_op=ALU.is_ge,
                                      fill=0.0, base=63, channel_multiplier=-1), writes=[tkC])
    sp.dma([(pm_sb[:], pm_h)], tkC, writes=[tkC])
    sp.dma([(gfin[:], gfin_h)], tkC, writes=[tkC])
    tkX0 = Tk("xcopy")
    allX = [t for row in tkX for t in row]
    sp.dma([(X[i * 512:(i + 1) * 512, :], xT_in[i * 512:(i + 1) * 512, :]) for i in range(4)], tkX0, writes=allX)
    K.barrier()

    def wload(pairs):
        slot, tk = WS.next()
        pool.dma([(f(slot), src) for (f, src) in pairs], tk, writes=[tk])
        return slot, tk

    def wview(slot, kc, ncols):
        return slot[:, 0:kc * ncols].rearrange("p (k n) -> p k n", k=kc)

    def wsrc(w2d, r0, nk, c0, ncols):
        return w2d[r0:r0 + nk * 128, c0:c0 + ncols].rearrange("(k p) n -> p k n", p=128)

    def mm_group(ps, ps_tk, lhs_list, rhs_list, reads, cols=None):
        n = len(lhs_list)
        out = ps[:] if cols is None else ps[:, cols]
        for i in range(n):
            pe.op(lambda h, o=out, a=lhs_list[i], b=rhs_list[i], st=(i == 0), sp_=(i == n - 1):
                  h.matmul(o, lhsT=a, rhs=b, start=st, stop=sp_),
                  reads=reads, writes=[ps_tk], inc=(i == n - 1))

    def resid_epilogue(n, gt, ps, ps_tk):
        xt, xtk = XT.next()
        rows = slice(n * 128, (n + 1) * 128)
        cols = slice(gt * 512, (gt + 1) * 512)
        sp.dma([(xt[:], X[rows, cols])], xtk, reads=[tkX[n][gt]], writes=[xtk])
        dve.op(lambda h: h.tensor_tensor(out=xt[:], in0=ps[:], in1=xt[:], op=ALU.add),
               reads=[ps_tk, xtk], writes=[xtk])
        sp.dma([(X[rows, cols], xt[:])], xtk, reads=[xtk], writes=[tkX[n][gt]])

    def norm_stage(seg, gvec, goff, to_out=False):
        for st in range(TS // 256):
            t0 = seg * TS + st * 256
            gt = t0 // 512
            xs, xtk = XS[st % 2], tkXS[st % 2]
            sp.dma([(xs, X[:, t0:t0 + 256].rearrange("(k p) t -> p k t", p=128))], xtk,
                   reads=[tkX[n][gt] for n in range(KC)], writes=[xtk])
            act.op(lambda h, xs=xs: h.activation(out=SQ, in_=xs, func=AF.Square), reads=[xtk], writes=[tkSQ])
            ps, ps_tk = K.bank()
            mm_group(ps, ps_tk, [ones_bf[:]] * KC, [SQ[:, k, :] for k in range(KC)], [tkSQ, tkC], cols=slice(0, 256))
            rs, rtk = T32.next()
            act.op(lambda h, ps=ps, rs=rs: h.activation(out=rs[:, 0:256], in_=ps[:, 0:256], func=AF.Sqrt,
                                                        bias=eps_t[:], scale=1.0 / D), reads=[ps_tk, tkC], writes=[rtk])
            dve.op(lambda h, rs=rs: h.reciprocal(out=rs[:, 256:512], in_=rs[:, 0:256]), reads=[rtk], writes=[rtk])
            for k in range(KC):
                if to_out:
                    dve.op(lambda h, k=k, xs=xs, rs=rs: h.scalar_tensor_tensor(
                        out=xs[:, k, :], in0=xs[:, k, :], scalar=gvec[:, goff + k:goff + k + 1], in1=rs[:, 256:512],
                        op0=ALU.mult, op1=ALU.mult), reads=[xtk, rtk, tkC, tkL], writes=[xtk])
                else:
                    dve.op(lambda h, k=k, xs=xs, rs=rs, st=st: h.scalar_tensor_tensor(
                        out=HT[:, k, st * 256:(st + 1) * 256], in0=xs[:, k, :], scalar=gvec[:, goff + k:goff + k + 1],
                        in1=rs[:, 256:512], op0=ALU.mult, op1=ALU.mult), reads=[xtk, rtk, tkC, tkL], writes=[tkHT])
            if to_out:
                sp.dma([(outT[:, t0:t0 + 256].rearrange("(k p) t -> p k t", p=128), xs)], xtk, reads=[xtk])

    def gemm_fm(w2d, c0, nchunks, kc, in_view, in_tks, epilogue, seg, group=4, r0=0):
        for g0 in range(0, nchunks, group):
            ng = min(group, nchunks - g0)
            ncols = ng * 128
            slot, stk = wload([(lambda s, ncols=ncols: wview(s, kc, ncols), wsrc(w2d, r0, kc, c0 + g0 * 128, ncols))])
            wv = wview(slot, kc, ncols)
            for j in range(ng):
                n = g0 + j
                for tt in range(NT):
                    ps, ps_tk = K.bank()
                    mm_group(ps, ps_tk, [wv[:, k, j * 128:(j + 1) * 128] for k in range(kc)],
                             [in_view(k, tt) for k in range(kc)], [stk] + in_tks)
                    epilogue(n, tt, ps, ps_tk)

    for l in range(L):
        Wi = w_in[l]
        for pair in [(gsgu[:], gsgu_h[l]), (bsp[:], bsp_h[l]), (wst_f[:], wsT_h[l]),
                     (gmix[:], gmix_h[:, l * KC:(l + 1) * KC]), (gffn[:], gffn_h[:, l * KC:(l + 1) * KC]),
                     (convw[:], convw_h[:, l * 3 * FC:(l + 1) * 3 * FC]), (convb[:], convb_h[:, l * FC:(l + 1) * FC]),
                     (bfor[:], bfor_h[:, l * H:(l + 1) * H])]:
            sp.dma([pair], tkL, writes=[tkL])
        pool.op(lambda h: h.affine_select(out=wst_f[:].rearrange("p (g t) -> p g t", g=H),
                                          in_=wst_f[:].rearrange("p (g t) -> p g t", g=H),
                                          pattern=[[0, H], [1, 128]], compare_op=ALU.is_ge, fill=0.0, base=0,
                                          channel_multiplier=-1), reads=[tkL], writes=[tkL])
        dve.op(lambda h: h.tensor_copy(out=wst[:], in_=wst_f[:]), reads=[tkL], writes=[tkL])
        pool.dma([(wf[:].rearrange("p (k n) -> p k n", k=KC), wsrc(Wi, 0, KC, OF, H))], tkWF, writes=[tkWF])
        dve.op(lambda h: h.memset(lsum[0][:], 0.0), writes=[tkLS[0]])
        ls_i = 0
        WST3 = wst[:].rearrange("p (g t) -> p g t", g=H)
        BSP3 = bsp[:].rearrange("p (g t) -> p g t", g=H)

        for seg in range(NSEG):
            T0 = seg * TS
            norm_stage(seg, gmix, 0)
            K.barrier()

            hin = lambda k, tt: HT[:, k, tt * 512:(tt + 1) * 512]

            def ep_q(n, tt, ps, ps_tk):
                ot, otk = OT.next()
                act.op(lambda h: h.activation(out=ot[:], in_=ps[:], func=AF.Copy, scale=DH ** -0.5),
                       reads=[ps_tk], writes=[otk])
                sp.dma([(QT[n * 128:(n + 1) * 128, tt * 512:(tt + 1) * 512], ot[:])], otk, reads=[otk], writes=[tkQT[n]])

            def ep_k(n, tt, ps, ps_tk):
                ot, otk = OT.next()
                dve.op(lambda h: h.tensor_copy(out=ot[:], in_=ps[:]), reads=[ps_tk], writes=[otk])
                sp.dma([(KTl[l][n // 2][(n % 2) * 128:(n % 2 + 1) * 128, tt * 512:(tt + 1) * 512], ot[:])], otk,
                       reads=[otk], writes=[tkKT[n]])

            def ep_u(n, tt, ps, ps_tk):
                act.op(lambda h: h.activation(out=UT[:, n, tt * 512:(tt + 1) * 512], in_=ps[:], func=AF.Gelu_apprx_tanh),
                       reads=[ps_tk], writes=[tkUT])

            def mk_ep_gate(G, tks):
                def ep(n, tt, ps, ps_tk):
                    ot, otk = OT.next()
                    act.op(lambda h: h.activation(out=ot[:], in_=ps[:], func=AF.Sigmoid), reads=[ps_tk], writes=[otk])
                    sp.dma([(G[n * 128:(n + 1) * 128, tt * 512:(tt + 1) * 512], ot[:])], otk, reads=[otk], writes=[tks[n]])
                return ep

            gemm_fm(Wi, OK_, H, KC, hin, [tkHT], ep_k, seg)
            for c in range(4):
                pool.cc(PAIRS, KTl[l][c], KTg[l][c], tkKTg[l][c], reads=[tkKT[2 * c], tkKT[2 * c + 1]], writes=[tkKTg[l][c]])
            for cg in range(2):
                slot, stk = wload([(lambda s: wview(s, KC, 512), wsrc(Wi, 0, KC, OV + cg * 512, 512))])
                wv = wview(slot, KC, 512)
                for b in range(NB):
                    ps, ps_tk = K.bank()
                    mm_group(ps, ps_tk, [HT[:, k, b * 128:(b + 1) * 128] for k in range(KC)],
                             [wv[:, k, :] for k in range(KC)], [stk, tkHT])
                    ot, otk = OT.next()
                    dve.op(lambda h, ot=ot, ps=ps: h.tensor_copy(out=ot[:], in_=ps[:]), reads=[ps_tk], writes=[otk])
                    sp.dma([(Vl[l][b // 4][(b % 4) * 128:(b % 4 + 1) * 128, cg * 512:(cg + 1) * 512], ot[:])], otk,
                           reads=[otk], writes=[tkVc[b // 4]])
            for c in range(4):
                pool.cc(PAIRS, Vl[l][c], Vg[l][c], tkVg[l][c], reads=[tkVc[c]], writes=[tkVg[l][c]])
            gemm_fm(Wi, OQ, H, KC, hin, [tkHT], ep_q, seg)
            gemm_fm(Wi, OU, H, KC, hin, [tkHT], ep_u, seg)
            gemm_fm(Wi, OGA, KC, KC, hin, [tkHT], mk_ep_gate(GA, tkGA), seg)
            gemm_fm(Wi, OGB, KC, KC, hin, [tkHT], mk_ep_gate(GB, tkGB), seg)
            slotA, stkA = wload([(lambda s: wview(s, KC, 512), wsrc(Wi, 0, KC, OVG, 512))])
            slotB, stkB = wload([(lambda s: wview(s, KC, 512), wsrc(Wi, 0, KC, OVG + 512, 512))])
            wvA, wvB = wview(slotA, KC, 512), wview(slotB, KC, 512)
            for b in range(NB):
                vg, vtk = VGF.next()
                for (wv, stk, c0) in ((wvA, stkA, 0), (wvB, stkB, 512)):
                    ps, ps_tk = K.bank()
                    mm_group(ps, ps_tk, [HT[:, k, b * 128:(b + 1) * 128] for k in range(KC)],
                             [wv[:, k, :] for k in range(KC)], [stk, tkHT])
                    act.op(lambda h, vg=vg, ps=ps, c0=c0: h.activation(out=vg[:, c0:c0 + 512], in_=ps[:], func=AF.Gelu_apprx_tanh),
                           reads=[ps_tk], writes=[vtk])
                sm, smk = SM.next()
                junk, jtk = T32.next()
                act.op(lambda h, vg=vg, sm=sm, junk=junk: h.activation(out=junk[:], in_=vg[:, 0:512], func=AF.Square,
                                                                       accum_out=sm[:, 0:1]), reads=[vtk], writes=[jtk, smk])
                act.op(lambda h, vg=vg, sm=sm, junk=junk: h.activation(out=junk[:], in_=vg[:, 512:1024], func=AF.Square,
                                                                       accum_out=sm[:, 1:2]), reads=[vtk], writes=[jtk, smk])
                dve.op(lambda h, sm=sm: h.tensor_tensor(out=sm[:, 2:3], in0=sm[:, 0:1], in1=sm[:, 1:2], op=ALU.add),
                       reads=[smk], writes=[smk])
                act.op(lambda h, sm=sm: h.activation(out=sm[:, 3:4], in_=sm[:, 2:3], func=AF.Sqrt, bias=eps_t[:], scale=1.0 / FW),
                       reads=[smk, tkC], writes=[smk])
                dve.op(lambda h, sm=sm: h.reciprocal(out=sm[:, 4:5], in_=sm[:, 3:4]), reads=[smk], writes=[smk])
                dve.op(lambda h, vg=vg, sm=sm, b=b: h.scalar_tensor_tensor(out=VCN[:, b, :], in0=vg[:], scalar=sm[:, 4:5],
                                                                             in1=gsgu[:], op0=ALU.mult, op1=ALU.mult),
                       reads=[vtk, smk, tkL], writes=[tkVCN])
            WF3 = wf[:].rearrange("p (k n) -> p k n", k=KC)
            for b in range(NB):
                gb = seg * NB + b
                ps, ps_tk = K.bank()
                mm_group(ps, ps_tk, [HT[:, k, b * 128:(b + 1) * 128] for k in range(KC)],
                         [WF3[:, k, :] for k in range(KC)], [tkWF, tkHT], cols=slice(0, H))
                sm, smk = SM.next()
                dve.op(lambda h, sm=sm, ps=ps: h.tensor_tensor(out=sm[:, 0:H], in0=ps[:, 0:H], in1=bfor[:, 0:H], op=ALU.add),
                       reads=[ps_tk, tkL], writes=[smk])
                act.op(lambda h, sm=sm: h.activation(out=sm[:, 0:H], in_=sm[:, 0:H], func=AF.Exp, scale=-1.0), reads=[smk], writes=[smk])
                act.op(lambda h, sm=sm: h.activation(out=sm[:, H:2 * H], in_=sm[:, 0:H], func=AF.Ln, bias=one_t[:], scale=1.0),
                       reads=[smk, tkC], writes=[smk])
                l1 = sm[:, H:2 * H]
                lo, lotk = lsum[ls_i], tkLS[ls_i]
                ln_, lntk = lsum[1 - ls_i], tkLS[1 - ls_i]
                ps2, ps2_tk = K.bank()
                mm_group(ps2, ps2_tk, [uneg[:], onesneg[:]], [l1, lo[:]], [smk, lotk, tkC], cols=slice(0, H))
                mm_group(ps2, ps2_tk, [halfneg[:], onesneg[:]], [l1, lo[:]], [smk, lotk, tkC], cols=slice(H, 2 * H))
                dve.op(lambda h, ps2=ps2, gb=gb: h.tensor_copy(out=CALL3[:, gb, :], in_=ps2[:, 0:H]), reads=[ps2_tk], writes=[tkCALL])
                dve.op(lambda h, ps2=ps2, gb=gb: h.tensor_copy(out=CREF3[:, gb, :], in_=ps2[:, H:2 * H]), reads=[ps2_tk], writes=[tkCALL])
                dve.op(lambda h, ln_=ln_, lo=lo, l1=l1: h.tensor_tensor(out=ln_[:], in0=lo[:], in1=l1, op=ALU.add),
                       reads=[smk, lotk], writes=[lntk])
                ls_i = 1 - ls_i
            K.barrier()
            psT, psT_tk = K.bank()
            mm_group(psT, psT_tk, [onesneg[:]], [lsum[ls_i][:]], [tkLS[ls_i], tkC], cols=slice(0, H))
            sm, smk = SM.next()
            dve.op(lambda h, sm=sm, psT=psT: h.tensor_copy(out=sm[:, 0:H], in_=psT[:, 0:H]), reads=[psT_tk], writes=[smk])
            for b in range(NB):
                dve.op(lambda h, b=b, sm=sm: h.tensor_tensor(out=CREL3[:, b, :], in0=CALL3[:, b, :], in1=sm[:, 0:H], op=ALU.subtract),
                       reads=[tkCALL, smk], writes=[tkCREL])
            sp.dma([(CRl[l], crel[:])], tkCREL, reads=[tkCREL], writes=[tkCRl[l]])
            pool.cc(PAIRS, CRl[l], CRg[l], tkCRg[l], reads=[tkCRl[l]], writes=[tkCRg[l]])
            sp.dma([(crelg[:], CRg[l][0:128, :])], tkCRELG, reads=[tkCRg[l]], writes=[tkCRELG])
            if debug:
                sp.dma([(dbg[(l * 4 + 0) * 128:(l * 4 + 1) * 128, :], crel[:])], tkCREL, reads=[tkCREL], writes=[tkDBG])
                sp.dma([(dbg[(l * 4 + 1) * 128:(l * 4 + 2) * 128, :], crelg[:])], tkCRELG, reads=[tkCRELG], writes=[tkDBG])

            for g in range(H):
                for tt in range(NT):
                    ps, ps_tk = K.bank()
                    for b4 in range(4):
                        b = tt * 4 + b4
                        pe.op(lambda h, ps=ps, b=b, b4=b4, g=g: h.matmul(ps[:, b4 * 128:(b4 + 1) * 128],
                                                                         lhsT=VCN[:, b, g * 128:(g + 1) * 128],
                                                                         rhs=WST3[:, g, :], start=True, stop=True),
                              reads=[tkVCN, tkL], writes=[ps_tk], inc=(b4 == 3))
                    tmp, ttk = T32.next()
                    for b4 in range(4):
                        dve.op(lambda h, tmp=tmp, ps=ps, b4=b4, g=g: h.tensor_tensor(
                            out=tmp[:, b4 * 128:(b4 + 1) * 128], in0=ps[:, b4 * 128:(b4 + 1) * 128], in1=BSP3[:, g, :], op=ALU.add),
                            reads=[ps_tk, tkL], writes=[ttk])
                    dve.op(lambda h, tmp=tmp, g=g, tt=tt: h.tensor_tensor(out=UT[:, g, tt * 512:(tt + 1) * 512], in0=tmp[:],
                                                                          in1=UT[:, g, tt * 512:(tt + 1) * 512], op=ALU.mult),
                           reads=[ttk, tkUT], writes=[tkUT])

            for hd in (range(H) if 'attn' not in SKIP else []):
                bi = hd % 2
                q_sb, k_sb, v_sb = ATQ[bi], ATK[bi], ATV[bi]
                hr = slice(hd * 128, (hd + 1) * 128)
                sp.dma([(q_sb, QT[hr, :])], tkATQ[bi], reads=[tkQT[hd]], writes=[tkATQ[bi]])
                h2 = slice((hd % 2) * 128, (hd % 2 + 1) * 128)
                sp.dma([(k_sb[:, TS:2 * TS], KTl[l][hd // 2][h2, :])], tkATK[bi], reads=[tkKT[hd]], writes=[tkATK[bi]])
                sp.dma([(v_sb[:, 16 + 4 * c:20 + 4 * c, :], Vl[l][c][:, hr].rearrange("(b p) d -> p b d", p=128)) for c in range(4)],
                       tkATV[bi], reads=tkVc, writes=[tkATV[bi]])
                sp.dma([(k_sb[:, 0:TS], KTg[l][hd // 2][h2, :])], tkATKp[bi], reads=[tkKTg[l][hd // 2]], writes=[tkATKp[bi]])
                sp.dma([(v_sb[:, 4 * c:4 * c + 4, :], Vg[l][c][0:512, hr].rearrange("(b p) d -> p b d", p=128)) for c in range(4)],
                       tkATVp[bi], reads=tkVg[l], writes=[tkATVp[bi]])
                if debug and hd == H - 1:
                    stg, stgk = T32.next()
                    dve.op(lambda h, stg=stg, k_sb=k_sb: h.tensor_copy(out=stg[:, 0:32], in_=k_sb[:, TS - 32:TS]), reads=[tkATKp[bi]], writes=[stgk])
                    dve.op(lambda h, stg=stg, k_sb=k_sb: h.tensor_copy(out=stg[:, 32:64], in_=k_sb[:, 2 * TS - 32:2 * TS]), reads=[tkATK[bi]], writes=[stgk])
                    dve.op(lambda h, stg=stg, v_sb=v_sb: h.tensor_copy(out=stg[:, 64:96], in_=v_sb[:, 15, 0:32]), reads=[tkATVp[bi]], writes=[stgk])
                    dve.op(lambda h, stg=stg, v_sb=v_sb: h.tensor_copy(out=stg[:, 96:128], in_=v_sb[:, 31, 0:32]), reads=[tkATV[bi]], writes=[stgk])
                    sp.dma([(dbg[(l * 4 + 2) * 128:(l * 4 + 3) * 128, 64:128], stg[:, 0:64]),
                            (dbg[(l * 4 + 3) * 128:(l * 4 + 4) * 128, 64:128], stg[:, 64:128])], stgk, reads=[stgk], writes=[tkDBG])
                for qt in range(NT):
                    gq0 = 16 + qt * 4
                    order = list(range(16, gq0 + 4)) + list(range(16))
                    nko = len(order)
                    psO, psO_tk = K.abank()
                    psD, psD_tk = K.abank()

                    def emit_od(st):
                        (idx, kb, cols, p, ptk, vtk_) = st
                        pe.op(lambda h, psO=psO, cols=cols, kb=kb, p=p, v_sb=v_sb, idx=idx, nko=nko: h.matmul(
                            psO[:, cols], lhsT=v_sb[:, kb, :], rhs=p[:, cols], start=(idx == 0), stop=(idx == nko - 1)),
                            reads=[vtk_, ptk], writes=[psO_tk], inc=False)
                        pe.op(lambda h, psD=psD, cols=cols, p=p, idx=idx, nko=nko: h.matmul(
                            psD[:, cols], lhsT=ones_bf[:], rhs=p[:, cols], start=(idx == 0), stop=(idx == nko - 1)),
                            reads=[ptk, tkC], writes=[psD_tk])

                    d4, d4k = biasb.next()
                    dve.op(lambda h, d4=d4, qt=qt, hd=hd: h.tensor_scalar(
                        out=d4[:, 0:4], in0=CREF3[:, qt * 4:qt * 4 + 4, hd], scalar1=CREF3[:, qt * 4, hd:hd + 1],
                        scalar2=None, op0=ALU.subtract), reads=[tkCALL], writes=[d4k])
                    s4t, s4k = S4.next()
                    for j in range(1, 4):
                        act.op(lambda h, s4t=s4t, d4=d4, j=j: h.activation(
                            out=s4t[:, j * 128:(j + 1) * 128], in_=zeros_bf[:], func=AF.Exp, bias=d4[:, j:j + 1], scale=1.0),
                            reads=[d4k, tkC], writes=[s4k])
                    pend = []
                    for idx, kb in enumerate(order):
                        own = kb >= 16
                        ktk = tkATK[bi] if own else tkATKp[bi]
                        vtk_ = tkATV[bi] if own else tkATVp[bi]
                        j0 = max(0, kb - gq0)
                        cols = slice(j0 * 128, 512)
                        bb, bbk = biasb.next()
                        if own:
                            dve.op(lambda h, bb=bb, kb=kb, qt=qt, hd=hd: h.tensor_scalar(
                                out=bb[:, 0:4], in0=CREF3[:, qt * 4:qt * 4 + 4, hd], scalar1=CALL3[:, kb - 16, hd:hd + 1],
                                scalar2=None, op0=ALU.subtract), reads=[tkCALL], writes=[bbk])
                        else:
                            dve.op(lambda h, bb=bb, kb=kb, qt=qt, hd=hd: h.tensor_scalar(
                                out=bb[:, 0:1], in0=CREF3[:, qt * 4:qt * 4 + 1, hd], scalar1=CRELG3[:, kb, hd:hd + 1],
                                scalar2=pmask[:, 0:1], op0=ALU.subtract, op1=ALU.add), reads=[tkCALL, tkCRELG, tkC], writes=[bbk])
                        psS, psS_tk = K.bank()
                        pe.op(lambda h, psS=psS, cols=cols, kb=kb, qt=qt, j0=j0, k_sb=k_sb, q_sb=q_sb: h.matmul(
                            psS[:, cols], lhsT=k_sb[:, kb * 128:(kb + 1) * 128],
                            rhs=q_sb[:, qt * 512 + j0 * 128:(qt + 1) * 512], start=True, stop=True),
                            reads=[ktk, tkATQ[bi]], writes=[psS_tk])
                        p, ptk = PT.next()
                        if own:
                            for j in range(j0, 4):
                                act.op(lambda h, p=p, psS=psS, j=j, bb=bb: h.activation(
                                    out=p[:, j * 128:(j + 1) * 128], in_=psS[:, j * 128:(j + 1) * 128], func=AF.Exp,
                                    bias=bb[:, j:j + 1], scale=1.0), reads=[psS_tk, bbk], writes=[ptk])
                        else:
                            act.op(lambda h, p=p, psS=psS, bb=bb: h.activation(
                                out=p[:], in_=psS[:], func=AF.Exp, bias=bb[:, 0:1], scale=1.0), reads=[psS_tk, bbk], writes=[ptk])
                            pool.op(lambda h, p=p, s4t=s4t: h.tensor_tensor(
                                out=p[:, 128:512], in0=p[:, 128:512], in1=s4t[:, 128:512], op=ALU.mult),
                                reads=[ptk, s4k], writes=[ptk])
                        if kb >= gq0:
                            dve.op(lambda h, p=p, j0=j0: h.tensor_tensor(out=p[:, j0 * 128:(j0 + 1) * 128],
                                                                          in0=p[:, j0 * 128:(j0 + 1) * 128], in1=mask_bf[:], op=ALU.mult),
                                   reads=[ptk, tkC], writes=[ptk])
                        pend.append((idx, kb, cols, p, ptk, vtk_))
                        if len(pend) > 2:
                            emit_od(pend.pop(0))
                    while pend:
                        emit_od(pend.pop(0))
                    rd, rdk = T32.next()
                    dve.op(lambda h, rd=rd, psD=psD: h.reciprocal(out=rd[:], in_=psD[:]), reads=[psD_tk], writes=[rdk])
                    dve.op(lambda h, rd=rd, psO=psO, hd=hd, qt=qt: h.tensor_tensor(
                        out=YA[:, hd, qt * 512:(qt + 1) * 512], in0=psO[:], in1=rd[:], op=ALU.mult),
                        reads=[psO_tk, rdk], writes=[tkYA])
            K.barrier()

            Wa, Wb = w_ba[l], w_bb[l]
            for g0 in (range(0, KC, 4) if 'merge' not in SKIP else []):
                slot, stk = wload([(lambda s: wview(s, 16, 512)[:, 0:8, :], wsrc(Wa, 0, 8, g0 * 128, 512)),
                                   (lambda s: wview(s, 16, 512)[:, 8:16, :], wsrc(Wb, 0, 8, g0 * 128, 512))])
                wv = wview(slot, 16, 512)
                for j in range(4):
                    n = g0 + j
                    for tt in range(NT):
                        psA, psA_tk = K.bank()
                        psB, psB_tk = K.bank()
                        mm_group(psA, psA_tk, [wv[:, k, j * 128:(j + 1) * 128] for k in range(8)],
                                 [YA[:, k, tt * 512:(tt + 1) * 512] for k in range(8)], [stk, tkYA])
                        mm_group(psB, psB_tk, [wv[:, 8 + k, j * 128:(j + 1) * 128] for k in range(8)],
                                 [YB[:, k, tt * 512:(tt + 1) * 512] for k in range(8)], [stk, tkUT])
                        ga, gatk = GTL.next()
                        gb_, gbtk = GTL.next()
                        sp.dma([(ga[:], GA[n * 128:(n + 1) * 128, tt * 512:(tt + 1) * 512])], gatk, reads=[tkGA[n]], writes=[gatk])
                        sp.dma([(gb_[:], GB[n * 128:(n + 1) * 128, tt * 512:(tt + 1) * 512])], gbtk, reads=[tkGB[n]], writes=[gbtk])
                        t1, t1k = T32.next()
                        t2, t2k = T32.next()
                        dve.op(lambda h, t1=t1, psA=psA, ga=ga: h.tensor_tensor(out=t1[:], in0=psA[:], in1=ga[:], op=ALU.mult),
                               reads=[psA_tk, gatk], writes=[t1k])
                        dve.op(lambda h, t2=t2, psB=psB, gb_=gb_: h.tensor_tensor(out=t2[:], in0=psB[:], in1=gb_[:], op=ALU.mult),
                               reads=[psB_tk, gbtk], writes=[t2k])
                        dve.op(lambda h, t1=t1, t2=t2, n=n, tt=tt: h.tensor_tensor(
                            out=MERGED[:, n, tt * 512:(tt + 1) * 512], in0=t1[:], in1=t2[:], op=ALU.add),
                            reads=[t1k, t2k], writes=[tkHT])

            gemm_fm(w_out[l], 0, KC, KC, hin, [tkHT],
                    lambda n, tt, ps, ps_tk: resid_epilogue(n, seg * NT + tt, ps, ps_tk), seg)

            if debug:
                tkd = Tk("dbgx%da" % l)
                sp.dma([(dbgx[(2 * l) * D:(2 * l + 1) * D, :], X[:, 0:128])], tkd, reads=[tkX[n][0] for n in range(KC)])
            XTL3 = xtl_sb[:].rearrange("p (k t) -> p k t", k=KC)
            XTG3 = xtg_sb[:].rearrange("p (k t) -> p k t", k=KC)
            sp.dma([(XTL3, X[:, TS - 2:TS].rearrange("(k p) t -> p k t", p=128))], tkXTL,
                   reads=[tkX[n][NT - 1] for n in range(KC)], writes=[tkXTL])
            sp.dma([(XTl[l], xtl_sb[:])], tkXTL, reads=[tkXTL], writes=[tkXTl[l]])
            pool.cc(PAIRS, XTl[l], XTg[l], tkXTg[l], reads=[tkXTl[l]], writes=[tkXTg[l]])
            sp.dma([(xtg_sb[:], XTg[l][0:128, :])], tkXTGs, reads=[tkXTg[l]], writes=[tkXTGs])
            if debug:
                sp.dma([(dbg[(l * 4 + 2) * 128:(l * 4 + 3) * 128, 0:32], xtl_sb[:])], tkXTL, reads=[tkXTL], writes=[tkDBG])
                sp.dma([(dbg[(l * 4 + 3) * 128:(l * 4 + 4) * 128, 0:32], xtg_sb[:])], tkXTGs, reads=[tkXTGs], writes=[tkDBG])
            norm_stage(seg, gffn, 0)
            sqx, sqxk = OT.next()
            act.op(lambda h, sqx=sqx: h.activation(out=sqx[:, 0:32], in_=xtg_sb[:], func=AF.Square), reads=[tkXTGs], writes=[sqxk])
            psX, psX_tk = K.bank()
            mm_group(psX, psX_tk, [ones_bf[:]] * KC, [sqx[:, 2 * k:2 * k + 2] for k in range(KC)], [sqxk, tkC], cols=slice(0, 2))
            smx, smxk = SM.next()
            act.op(lambda h, smx=smx, psX=psX: h.activation(out=smx[:, 0:2], in_=psX[:, 0:2], func=AF.Sqrt, bias=eps_t[:], scale=1.0 / D),
                   reads=[psX_tk, tkC], writes=[smxk])
            dve.op(lambda h, smx=smx: h.reciprocal(out=smx[:, 2:4], in_=smx[:, 0:2]), reads=[smxk], writes=[smxk])
            for k in range(KC):
                dve.op(lambda h, k=k, smx=smx: h.scalar_tensor_tensor(out=HX3[:, k, :], in0=XTG3[:, k, :], scalar=gffn[:, k:k + 1],
                                                             in1=smx[:, 2:4], op0=ALU.mult, op1=ALU.mult),
                       reads=[tkXTGs, smxk, tkL], writes=[tkHX])
            K.barrier()

            Wu = w_up[l]
            CW = convw[:].rearrange("p (t f) -> p t f", t=3)
            CB = convb[:]
            for g0 in (range(0, FC, 2) if 'ffnup' not in SKIP else []):
                slot, stk = wload([(lambda s: wview(s, KC, 512)[:, :, 0:256], wsrc(Wu, 0, KC, g0 * 128, 256)),
                                   (lambda s: wview(s, KC, 512)[:, :, 256:512], wsrc(Wu, 0, KC, DFF + g0 * 128, 256))])
                wv = wview(slot, KC, 512)
                for j in range(2):
                    n = g0 + j
                    asb, atk = ASB[n % 2], tkASB[n % 2]
                    for tt in range(NT):
                        psA, psA_tk = K.bank()
                        psB, psB_tk = K.bank()
                        mm_group(psA, psA_tk, [wv[:, k, j * 128:(j + 1) * 128] for k in range(KC)],
                                 [hin(k, tt) for k in range(KC)], [stk, tkHT])
                        mm_group(psB, psB_tk, [wv[:, k, 256 + j * 128:256 + (j + 1) * 128] for k in range(KC)],
                                 [hin(k, tt) for k in range(KC)], [stk, tkHT])
                        c0 = tt * 512
                        if tt == 0:
                            psH, psH_tk = K.abank()
                            mm_group(psH, psH_tk, [wv[:, k, j * 128:(j + 1) * 128] for k in range(KC)],
                                     [HX3[:, k, :] for k in range(KC)], [stk, tkHX], cols=slice(0, 2))
                            dve.op(lambda h, asb=asb, psH=psH: h.tensor_scalar(out=asb[:, 0:2], in0=psH[:, 0:2], scalar1=pflag[:, 0:1],
                                                                               scalar2=None, op0=ALU.mult),
                                   reads=[psH_tk, tkC], writes=[atk])
                        act.op(lambda h, asb=asb, psA=psA, c0=c0: h.activation(out=asb[:, 2 + c0:2 + c0 + 512], in_=psA[:], func=AF.Copy),
                               reads=[psA_tk], writes=[atk])
                        acc, acck = ACC[tt % 2], tkACC[tt % 2]
                        dve.op(lambda h, acc=acc, asb=asb, c0=c0, n=n: h.tensor_scalar(
                            out=acc, in0=asb[:, 2 + c0:2 + c0 + 512], scalar1=CW[:, 2, n:n + 1], scalar2=CB[:, n:n + 1],
                            op0=ALU.mult, op1=ALU.add), reads=[atk, tkL], writes=[acck])
                        dve.op(lambda h, acc=acc, asb=asb, c0=c0, n=n: h.scalar_tensor_tensor(
                            out=acc, in0=asb[:, 1 + c0:1 + c0 + 512], scalar=CW[:, 1, n:n + 1], in1=acc,
                            op0=ALU.mult, op1=ALU.add), reads=[atk, tkL], writes=[acck])
                        dve.op(lambda h, acc=acc, asb=asb, c0=c0, n=n: h.scalar_tensor_tensor(
                            out=acc, in0=asb[:, c0:c0 + 512], scalar=CW[:, 0, n:n + 1], in1=acc,
                            op0=ALU.mult, op1=ALU.add), reads=[atk, tkL], writes=[acck])
                        gg, ggk = GG[tt % 2], tkGG[tt % 2]
                        act.op(lambda h, gg=gg, acc=acc: h.activation(out=gg, in_=acc, func=AF.Gelu_apprx_tanh),
                               reads=[acck], writes=[ggk])
                        ot, otk = OT.next()
                        dve.op(lambda h, ot=ot, gg=gg, psB=psB: h.tensor_tensor(out=ot[:], in0=psB[:], in1=gg, op=ALU.mult),
                               reads=[psB_tk, ggk], writes=[otk])
                        sp.dma([(GT[n * 128:(n + 1) * 128, c0:c0 + 512], ot[:])], otk, reads=[otk], writes=[tkGT[n]])
            K.barrier()

            Wd = w_down[l]
            for half in (range(2) if 'ffndn' not in SKIP else []):
                sp.dma([(GTH, GT[half * 2816:(half + 1) * 2816, :].rearrange("(k p) t -> p k t", p=128))], tkGTH,
                       reads=tkGT[half * 22:(half + 1) * 22], writes=[tkGTH])
                gin = lambda k, tt: GTH[:, k, tt * 512:(tt + 1) * 512]
                gemm_fm(Wd, 0, KC, 22, gin, [tkGTH],
                        lambda n, tt, ps, ps_tk: resid_epilogue(n, seg * NT + tt, ps, ps_tk), seg, group=2, r0=half * 2816)
            K.barrier()

        if debug:
            tkd = Tk("dbgx%db" % l)
            sp.dma([(dbgx[(2 * l + 1) * D:(2 * l + 2) * D, :], X[:, 0:128])], tkd, reads=[tkX[n][0] for n in range(KC)])
    for seg in range(NSEG):
        if final_norm:
            norm_stage(seg, gfin, 0, to_out=True)
        else:
            pass
    if not final_norm:
        tko = Tk("ocopy")
        sp.dma([(outT[i * 512:(i + 1) * 512, :], X[i * 512:(i + 1) * 512, :]) for i in range(4)], tko,
               reads=[t for row in tkX for t in row])
    return K.finish()


def _pm(v, nchunk):
    return np.ascontiguousarray(v.reshape(nchunk, 128).T)


def host_layout(L, g_mix, b_forget, g_sgu, w_spatial, b_spatial, g_ffn, conv_w, conv_b, g_final):
    f = np.float32
    m = {}
    m["gmix_h"] = np.ascontiguousarray(np.concatenate([_pm(g_mix[l], KC) for l in range(L)], 1), f)
    m["gffn_h"] = np.ascontiguousarray(np.concatenate([_pm(g_ffn[l], KC) for l in range(L)], 1), f)
    m["gfin_h"] = np.ascontiguousarray(_pm(g_final, KC), f)
    m["convw_h"] = np.ascontiguousarray(np.concatenate([_pm(conv_w[l, t], FC) for l in range(L) for t in range(3)], 1), f)
    m["convb_h"] = np.ascontiguousarray(np.concatenate([_pm(conv_b[l], FC) for l in range(L)], 1), f)
    m["bfor_h"] = np.ascontiguousarray(np.broadcast_to(b_forget[:L].reshape(1, L * H), (128, L * H)), f)
    m["gsgu_h"] = np.ascontiguousarray(np.broadcast_to(g_sgu[:L, None, :], (L, 128, FW)), f)
    m["bsp_h"] = np.ascontiguousarray(np.broadcast_to(b_spatial[:L].reshape(L, 1, H * 128), (L, 128, H * 128)), f)
    m["wsT_h"] = np.ascontiguousarray(np.transpose(w_spatial[:L], (0, 3, 1, 2)).reshape(L, 128, H * 128), f)
    return m


_NC_CACHE = {}
DEBUG = False
SKIP = set()


def kernel(x, g_mix, w_in, b_forget, g_sgu, w_spatial, b_spatial, w_branch_a, w_branch_b, w_out,
           g_ffn, w_up, conv_w, conv_b, w_down, g_final):
    x = np.asarray(x)
    B, S, _ = x.shape
    assert S == 2 * TS and 2 * B <= 8
    L = int(np.asarray(w_in).shape[0])
    args = [np.asarray(a, dtype=np.float32) for a in (g_mix, b_forget, g_sgu, w_spatial, b_spatial, g_ffn, conv_w, conv_b, g_final)]
    base = host_layout(L, *args)
    for name, a in (("w_in", w_in), ("w_branch_a", w_branch_a), ("w_branch_b", w_branch_b), ("w_out", w_out),
                    ("w_up", w_up), ("w_down", w_down)):
        base[name] = np.ascontiguousarray(np.asarray(a, dtype=np.float32))
    key = (L,)
    if key not in _NC_CACHE:
        _NC_CACHE[key] = build(L, TS, debug=DEBUG)
    nc = _NC_CACHE[key]
    ncore = 8
    in_maps = []
    for c in range(ncore):
        b, r = (c // 2) % B, c % 2
        m = dict(base)
        m["xT"] = np.ascontiguousarray(x[b, r * TS:(r + 1) * TS].T.astype(np.float32))
        pm = np.zeros((128, 16), np.float32)
        pm[:, 0] = -30000.0 if r == 0 else 0.0
        pm[:, 1] = 0.0 if r == 0 else 1.0
        m["pm"] = pm
        in_maps.append(m)
    res = run_bass_kernel_spmd(nc, in_maps, core_ids=list(range(ncore)))
    out = np.empty((B, S, D), np.float32)
    for c in range(2 * B):
        b, r = c // 2, c % 2
        out[b, r * TS:(r + 1) * TS] = res.results[c]["outT"].T
    if DEBUG:
        kernel.dbg = [res.results[c]["dbg"] for c in range(ncore)]
        kernel.dbgx = [res.results[c]["dbgx"] for c in range(ncore)]
    return out
```
